# Optimizing a Trainium2 kernel written in Bass

```python
import jax, jax.numpy as jnp
from jax import lax
import numpy as np

D_MODEL = 2048
BATCH = 8
SEQ = 2048
DEPTH = 1

N_META = 16
BLOCK_Q = 128
CONV_CH = 2048
CONV_WIDTH = 31
N_HEADS = 16
Q_LORA = 512
KV_LORA = 512
QK_NOPE = 128
QK_ROPE = 64
V_HEAD = 128
ROPE_THETA = 10000.0
D_FF = -(-8 * D_MODEL // (3 * 256)) * 256
EPS = 1e-6
SPLITS = (2 * CONV_CH, Q_LORA, KV_LORA, QK_ROPE, 2 * D_MODEL)
IN_COLS = sum(SPLITS)
SPLIT_IDX = tuple(int(v) for v in np.cumsum(SPLITS)[:-1])

kernel_name = "hybrid_conformer_conv_mla_gated_block"


def rmsnorm(x, g):
    xf = x.astype(jnp.float32)
    y = xf * lax.rsqrt(jnp.mean(xf * xf, axis=-1, keepdims=True) + EPS)
    return y.astype(x.dtype) * g


def layernorm(x, g, b):
    xf = x.astype(jnp.float32)
    mu = jnp.mean(xf, axis=-1, keepdims=True)
    var = jnp.mean(jnp.square(xf - mu), axis=-1, keepdims=True)
    return ((xf - mu) * lax.rsqrt(var + EPS)).astype(x.dtype) * g + b


def rope(x, cos, sin):
    x1, x2 = jnp.split(x, 2, axis=-1)
    return jnp.concatenate([x1 * cos - x2 * sin, x1 * sin + x2 * cos], axis=-1)


def causal_depthwise_conv(x, w):
    return lax.conv_general_dilated(
        x, w[:, None, :], window_strides=(1,), padding=[(CONV_WIDTH - 1, 0)],
        dimension_numbers=("NWC", "WIO", "NWC"), feature_group_count=x.shape[-1])


def mla_attention(q_nope, q_rope, k_nope, k_rope, v):
    b, lp = q_nope.shape[0], q_nope.shape[1]
    n_blocks = lp // BLOCK_Q
    scale = (QK_NOPE + QK_ROPE) ** -0.5
    kpos = jnp.arange(lp)

    def one_block(i):
        start = i * BLOCK_Q
        qn = lax.dynamic_slice_in_dim(q_nope, start, BLOCK_Q, axis=1)
        qr = lax.dynamic_slice_in_dim(q_rope, start, BLOCK_Q, axis=1)
        s = (jnp.einsum("bqhd,bkhd->bhqk", qn, k_nope)
             + jnp.einsum("bqhr,bkr->bhqk", qr, k_rope)).astype(jnp.float32) * scale
        qpos = start + jnp.arange(BLOCK_Q)
        s = jnp.where(kpos[None, :] <= qpos[:, None], s, -jnp.inf)
        p = jax.nn.softmax(s, axis=-1).astype(v.dtype)
        return jnp.einsum("bhqk,bkhd->bqhd", p, v)

    o = lax.map(one_block, jnp.arange(n_blocks))
    return jnp.moveaxis(o, 0, 1).reshape(b, lp, N_HEADS * V_HEAD)


def setup_inputs(seed: int = 0) -> dict:
    key = jax.random.key(seed)
    ks = iter(jax.random.split(key, 32))
    f32 = jnp.float32

    def w(shape, fan_in):
        return jax.random.normal(next(ks), shape, f32) * (fan_in ** -0.5)

    def gain(shape):
        return 1.0 + 0.02 * jax.random.normal(next(ks), shape, f32)

    def bias(shape):
        return 0.02 * jax.random.normal(next(ks), shape, f32)

    Ld = DEPTH
    return {
        "x": jax.random.normal(next(ks), (BATCH, SEQ, D_MODEL), f32),
        "meta_tokens": jax.random.normal(next(ks), (N_META, D_MODEL), f32),
        "g_mix": gain((Ld, D_MODEL)),
        "w_in": w((Ld, D_MODEL, IN_COLS), D_MODEL),
        "b_glu": bias((Ld, 2 * CONV_CH)),
        "b_gate": bias((Ld, 2 * D_MODEL)),
        "w_dw": w((Ld, CONV_WIDTH, CONV_CH), CONV_WIDTH),
        "b_dw": bias((Ld, CONV_CH)),
        "g_conv_ln": gain((Ld, CONV_CH)),
        "b_conv_ln": bias((Ld, CONV_CH)),
        "w_conv_out": w((Ld, CONV_CH, D_MODEL), CONV_CH),
        "b_conv_out": bias((Ld, D_MODEL)),
        "g_q_lora": gain((Ld, Q_LORA)),
        "w_uq": w((Ld, Q_LORA, N_HEADS * (QK_NOPE + QK_ROPE)), Q_LORA),
        "g_kv_lora": gain((Ld, KV_LORA)),
        "w_uk": w((Ld, KV_LORA, N_HEADS * QK_NOPE), KV_LORA),
        "w_uv": w((Ld, KV_LORA, N_HEADS * V_HEAD), KV_LORA),
        "w_attn_out": w((Ld, N_HEADS * V_HEAD, D_MODEL), N_HEADS * V_HEAD),
        "w_out": w((Ld, D_MODEL, D_MODEL), D_MODEL),
        "g_ffn": gain((Ld, D_MODEL)),
        "w_ffn_gate": w((Ld, D_MODEL, D_FF), D_MODEL),
        "w_ffn_up": w((Ld, D_MODEL, D_FF), D_MODEL),
        "w_ffn_down": w((Ld, D_FF, D_MODEL), D_FF),
        "g_final": gain((D_MODEL,)),
    }


def reference(x, meta_tokens, g_mix, w_in, b_glu, b_gate, w_dw, b_dw, g_conv_ln, b_conv_ln,
              w_conv_out, b_conv_out, g_q_lora, w_uq, g_kv_lora, w_uk, w_uv, w_attn_out,
              w_out, g_ffn, w_ffn_gate, w_ffn_up, w_ffn_down, g_final):
    b, seq, d = x.shape
    length = N_META + seq
    lp = -(-length // BLOCK_Q) * BLOCK_Q
    meta = jnp.broadcast_to(meta_tokens[None].astype(x.dtype), (b, N_META, d))
    h = jnp.concatenate([meta, x], axis=1)
    h = jnp.pad(h, ((0, 0), (0, lp - length), (0, 0)))

    pos = jnp.arange(lp, dtype=jnp.float32)
    inv_freq = ROPE_THETA ** (-jnp.arange(0, QK_ROPE, 2, dtype=jnp.float32) / QK_ROPE)
    ang = pos[:, None] * inv_freq[None, :]
    cos, sin = jnp.cos(ang).astype(h.dtype), jnp.sin(ang).astype(h.dtype)

    for l in range(DEPTH):
        u = rmsnorm(h, g_mix[l])
        z = u @ w_in[l]
        z_glu, z_cq, z_ckv, z_kr, z_gate = jnp.split(z, SPLIT_IDX, axis=-1)

        za, zb = jnp.split(z_glu + b_glu[l], 2, axis=-1)
        c = za * jax.nn.sigmoid(zb)
        c = causal_depthwise_conv(c, w_dw[l]) + b_dw[l]
        c = jax.nn.silu(layernorm(c, g_conv_ln[l], b_conv_ln[l]))
        y_conv = c @ w_conv_out[l] + b_conv_out[l]

        cq = rmsnorm(z_cq, g_q_lora[l])
        q = (cq @ w_uq[l]).reshape(b, lp, N_HEADS, QK_NOPE + QK_ROPE)
        q_nope = q[..., :QK_NOPE]
        q_rope = rope(q[..., QK_NOPE:], cos[:, None, :], sin[:, None, :])
        ckv = rmsnorm(z_ckv, g_kv_lora[l])
        k_nope = (ckv @ w_uk[l]).reshape(b, lp, N_HEADS, QK_NOPE)
        v = (ckv @ w_uv[l]).reshape(b, lp, N_HEADS, V_HEAD)
        k_rope = rope(z_kr, cos, sin)
        y_attn = mla_attention(q_nope, q_rope, k_nope, k_rope, v) @ w_attn_out[l]

        g_c, g_a = jnp.split(jax.nn.sigmoid(z_gate + b_gate[l]), 2, axis=-1)
        h = h + (g_c * y_conv + g_a * y_attn) @ w_out[l]

        hn = rmsnorm(h, g_ffn[l])
        h = h + (jax.nn.silu(hn @ w_ffn_gate[l]) * (hn @ w_ffn_up[l])) @ w_ffn_down[l]

    h = rmsnorm(h, g_final)
    return h[:, N_META:N_META + seq]
```

```python
import numpy as np
import concourse.bass as bass
import concourse.mybir as mybir
from concourse.bass_utils import run_bass_kernel_spmd

F32, BF16 = mybir.dt.float32, mybir.dt.bfloat16
AF = mybir.ActivationFunctionType
ALU = mybir.AluOpType

D = 2048
NCH = 16
NMETA = 16
SEQ = 2048
L = NMETA + SEQ
LP = 2176
TILES = [(0, 768), (768, 768), (1536, 528)]
SL = 384
TB = 768
DFF = 5632
NFG = 11
NH = 16
SCALE = float(192 ** -0.5)
EPS = 1e-6
NW = 4
GR = 256


def stream_order():
    o = []
    for j in range(4):
        o.append(("in", "cq", j))
    for j in range(4):
        o.append(("in", "ckv", j))
    o.append(("in", "kr", 0))
    for j in range(16):
        o.append(("in", "za", j))
        o.append(("in", "zb", j))
    for g in range(4):
        o.append(("uv", g))
        o.append(("uk", g))
        o.append(("uq", 2 * g))
        o.append(("uq", 2 * g + 1))
    for j in range(16):
        o.append(("ao", j))
        o.append(("in", "ga", j))
    for j in range(16):
        o.append(("co", j))
        o.append(("in", "gc", j))
    for j in range(16):
        o.append(("wo", j))
    for g in range(NFG):
        for f in range(4):
            o.append(("fg", g * 4 + f))
            o.append(("fu", g * 4 + f))
        for cb in range(4):
            o.append(("fd", g, cb))
    return o


ORDER = stream_order()
NB = len(ORDER)

VOFF = {}
_c = 0
for _n, _w in [("b_za", 16), ("b_zb", 16), ("b_gc", 16), ("b_ga", 16), ("b_dw", 16), ("g_ln", 16), ("b_ln", 16),
               ("b_co", 16), ("g_q", 4), ("g_kv", 4), ("g_ffn", 16), ("g_fin", 16), ("eps", 1), ("w_dw", 31 * 16)]:
    VOFF[_n] = _c
    _c += _w
NV = _c


def _pack_k2048(W, cols):
    return np.ascontiguousarray(W[:, cols].reshape(16, 128, 128).transpose(1, 0, 2).reshape(128, 2048))


def _pack_k512(W, cols):
    return np.ascontiguousarray(W[:, cols].reshape(4, 128, 512).transpose(1, 0, 2).reshape(128, 2048))


def pack_weights(w_in, w_conv_out, w_uq, w_uk, w_uv, w_attn_out, w_out, w_ffn_gate, w_ffn_up, w_ffn_down):
    wst = np.empty((NB, 128, 2048), np.float32)
    ar = np.arange
    base = {"za": 0, "zb": 2048, "cq": 4096, "ckv": 4608, "gc": 5184, "ga": 7232}
    for i, d in enumerate(ORDER):
        t = d[0]
        if t == "in":
            if d[1] == "kr":
                r = 5120 + ar(64)
                sw = 5120 + np.concatenate([ar(32, 64), ar(0, 32)])
                cols = np.concatenate([r, sw])
            else:
                cols = base[d[1]] + d[2] * 128 + ar(128)
            wst[i] = _pack_k2048(w_in, cols)
        elif t == "co":
            wst[i] = _pack_k2048(w_conv_out, d[1] * 128 + ar(128))
        elif t == "ao":
            wst[i] = _pack_k2048(w_attn_out, d[1] * 128 + ar(128))
        elif t == "wo":
            wst[i] = _pack_k2048(w_out, d[1] * 128 + ar(128))
        elif t == "fg":
            wst[i] = _pack_k2048(w_ffn_gate, d[1] * 128 + ar(128))
        elif t == "fu":
            wst[i] = _pack_k2048(w_ffn_up, d[1] * 128 + ar(128))
        elif t == "fd":
            g, cb = d[1], d[2]
            blk = w_ffn_down[g * 512:(g + 1) * 512, cb * 512:(cb + 1) * 512]
            wst[i] = blk.reshape(4, 128, 512).transpose(1, 0, 2).reshape(128, 2048)
        elif t == "uv":
            wst[i] = _pack_k512(w_uv, d[1] * 512 + ar(512))
        elif t == "uk":
            wst[i] = _pack_k512(w_uk, d[1] * 512 + ar(512))
        elif t == "uq":
            cols = []
            for h in (2 * d[1], 2 * d[1] + 1):
                b = h * 192
                cols += [b + ar(128), b + 128 + ar(64), b + 128 + ar(32, 64), b + 128 + ar(0, 32)]
            wst[i] = _pack_k512(w_uq, np.concatenate(cols))
        else:
            raise ValueError(t)
    return wst


def _col(v):
    return np.ascontiguousarray(np.asarray(v, np.float32).reshape(-1, 128).T)


def pack_vecs(b_glu, b_gate, b_dw, g_conv_ln, b_conv_ln, b_conv_out, g_q_lora, g_kv_lora, g_ffn, g_final, w_dw):
    v = np.zeros((128, NV), np.float32)

    def put(name, arr):
        c = _col(arr)
        v[:, VOFF[name]:VOFF[name] + c.shape[1]] = c
    put("b_za", b_glu[:2048])
    put("b_zb", b_glu[2048:])
    put("b_gc", b_gate[:2048])
    put("b_ga", b_gate[2048:])
    put("b_dw", b_dw)
    put("g_ln", g_conv_ln)
    put("b_ln", b_conv_ln)
    put("b_co", b_conv_out)
    put("g_q", g_q_lora)
    put("g_kv", g_kv_lora)
    put("g_ffn", g_ffn)
    put("g_fin", g_final)
    v[:, VOFF["eps"]] = EPS
    for j in range(31):
        c = _col(w_dw[j])
        v[:, VOFF["w_dw"] + j * 16: VOFF["w_dw"] + (j + 1) * 16] = c
    return v


def const_tables():
    p = np.arange(128)
    ident = np.eye(128, dtype=np.float32)
    maskneg = np.where(p[:, None] <= p[None, :], 0.0, -30000.0).astype(np.float32)
    fold = (p[:, None] % 64 == p[None, :] % 64).astype(np.float32)
    consts = np.concatenate([ident, maskneg, fold], axis=1)
    pos = np.arange(LP, dtype=np.float32)
    inv_freq = (np.float32(10000.0) ** (-np.arange(0, 64, 2, dtype=np.float32) / np.float32(64))).astype(np.float32)
    ang = (pos[None, :] * inv_freq[:, None]).astype(np.float32)
    cos, sin = np.cos(ang).astype(np.float32), np.sin(ang).astype(np.float32)
    cs = np.concatenate([cos, cos, -sin, sin], axis=0).astype(np.float32)
    return np.ascontiguousarray(consts), np.ascontiguousarray(cs)


class R:
    __slots__ = ("ap", "keys")

    def __init__(self, ap, keys=()):
        self.ap = ap
        self.keys = frozenset(keys)


class Sched:
    def __init__(self):
        self.ops = {n: [] for n in ("pe", "act", "dve", "pool", "sp")}
        self.count = {n: 0 for n in self.ops}
        self.waited = {n: {} for n in self.ops}
        self.last_w = {}
        self.readers = {}
        self.dma_count = {}
        self.maxwait = {}

    def op(self, eng, fn, reads=(), writes=(), signal=True, dma_sem=None):
        deps = {}

        def need(ev):
            if ev is not None and deps.get(ev[0], 0) < ev[1]:
                deps[ev[0]] = ev[1]
        rk = set()
        for r in reads:
            rk |= r.keys
        wk = set()
        for w in writes:
            wk |= w.keys
        for k in rk:
            need(self.last_w.get(k))
        for k in wk:
            need(self.last_w.get(k))
            rd = self.readers.get(k)
            if rd:
                for s, v in rd.items():
                    need((s, v))
        if dma_sem is not None:
            self.dma_count[dma_sem] = self.dma_count.get(dma_sem, 0) + 16
            ev = (dma_sem, self.dma_count[dma_sem])
            signal = False
        elif signal:
            self.count[eng] += 1
            ev = (eng, self.count[eng])
        else:
            ev = (eng, self.count[eng] + 1)
        waits = []
        wd = self.waited[eng]
        for k, v in deps.items():
            if k == eng:
                continue
            if wd.get(k, 0) >= v:
                continue
            wd[k] = v
            waits.append((k, v))
            if self.maxwait.get(k, 0) < v:
                self.maxwait[k] = v
        self.ops[eng].append((fn, waits, signal, dma_sem))
        for k in rk:
            d = self.readers.setdefault(k, {})
            if d.get(ev[0], 0) < ev[1]:
                d[ev[0]] = ev[1]
        for k in wk:
            self.last_w[k] = ev
            self.readers[k] = {}
        return ev


class Buf:
    def __init__(self, arena, off, dtype, n0, n1, tag):
        self.off, self.dtype, self.n0, self.n1, self.tag = off, dtype, n0, n1, tag
        self.es = 4 if dtype == F32 else 2
        nb = n0 * n1 * self.es
        a = arena[:, off // 2: (off + nb) // 2]
        if dtype == F32:
            a = a.bitcast(F32)
        self.ap = a.rearrange("p (a b) -> p a b", a=n0)

    def _keys(self, i0, i1, c0, c1):
        ks = set()
        for i in range(i0, i1):
            lo = self.off + (i * self.n1 + c0) * self.es
            hi = self.off + (i * self.n1 + c1) * self.es
            ks.update(("sb", g) for g in range(lo // GR, (hi - 1) // GR + 1))
        return ks

    def v(self, i, c0=0, c1=None, p0=0, p1=128):
        c1 = self.n1 if c1 is None else c1
        return R(self.ap[p0:p1, i, c0:c1], self._keys(i, i + 1, c0, c1))

    def v3(self, i0, i1, c0=0, c1=None):
        c1 = self.n1 if c1 is None else c1
        return R(self.ap[:, i0:i1, c0:c1], self._keys(i0, i1, c0, c1))


def build_program(ntiles=3, dbg=None):
    nc = bass.Bass("TRN2", target_bir_lowering=False)
    h0 = nc.dram_tensor("h0", [L, D], F32, kind="ExternalInput").ap()
    wst = nc.dram_tensor("wst", [NB, 128, 2048], F32, kind="ExternalInput").ap()
    vecs_d = nc.dram_tensor("vecs", [128, NV], F32, kind="ExternalInput").ap()
    consts_d = nc.dram_tensor("consts", [128, 384], F32, kind="ExternalInput").ap()
    cs_d = nc.dram_tensor("cs", [128, LP], F32, kind="ExternalInput").ap()
    gmix_d = nc.dram_tensor("gmix", [D], F32, kind="ExternalInput").ap()
    out_d = nc.dram_tensor("out", [SEQ, D], F32, kind="ExternalOutput").ap()
    dbg_d = {}
    if dbg:
        for name, shape in dbg.items():
            dbg_d[name] = nc.dram_tensor("dbg_" + name, list(shape), F32, kind="ExternalOutput").ap()

    tiles = TILES[:ntiles]
    S = Sched()
    ARENA_BYTES = 207 * 1024 + 512
    import contextlib
    with contextlib.ExitStack() as es:
        arena = es.enter_context(nc.sbuf_tensor("arena", [128, ARENA_BYTES // 2], BF16))
        ps_t = es.enter_context(nc.psum_tensor("ps", [128, 8, 512], F32))
        cur = [0]

        def alloc(nbytes):
            o = cur[0]
            cur[0] = (o + nbytes + GR - 1) // GR * GR
            assert cur[0] <= ARENA_BYTES, ("SBUF overflow", cur[0])
            return o

        def mk(dtype, n0, n1, tag, off=None):
            es_ = 4 if dtype == F32 else 2
            if off is None:
                off = alloc(n0 * n1 * es_)
            return Buf(arena, off, dtype, n0, n1, tag)

        o_bb = alloc(2 * NCH * TB * 2)
        B1 = mk(BF16, NCH, TB, "B1", o_bb)
        B4 = mk(BF16, NCH, TB, "B4", o_bb + NCH * TB * 2)
        H = mk(F32, NCH, TB, "H", o_bb)
        CBW = 32 + TB
        CB = mk(BF16, NCH, CBW, "CB")
        B3 = mk(BF16, NCH, TB, "B3")
        CQ = mk(BF16, 4, TB, "CQ")
        CKV = mk(BF16, 4, LP, "CKV")
        KR = mk(BF16, 1, LP, "KR")
        o_sa = alloc(32768)
        XS = [mk(F32, 1, D, "XS%d" % i, o_sa + i * 8192) for i in range(2)]
        UTM = mk(BF16, 1, D, "UTM", o_sa + 16384)
        JUNK = mk(BF16, 1, D, "JUNK", o_sa + 20480)
        GMX = mk(F32, 1, D, "GMX", o_sa + 24576)
        ACTG = [mk(BF16, 4, TB, "ACTG%d" % i, o_sa + 16384 + i * 4 * TB * 2) for i in range(2)]
        VG = mk(BF16, 17, 512, "VG", o_sa)
        KH = mk(BF16, 1, LP, "KH", o_sa + 17408)
        QN = mk(BF16, 1, TB, "QN", o_sa + 17408 + 4352)
        QR = mk(BF16, 1, TB, "QR", o_sa + 17408 + 4352 + 1536)
        PT = [mk(BF16, 1, SL, "PT%d" % i, o_sa + 17408 + 4352 + 3072 + i * 768) for i in range(3)]
        RDEN = mk(F32, 1, SL, "RDEN", o_sa + 17408 + 4352 + 3072 + 2304)
        assert 17408 + 4352 + 3072 + 2304 + 1536 <= 32768
        CACC = [mk(F32, 1, TB, "CACC%d" % i) for i in range(2)]
        TMPS = [mk(F32, 1, SL, "TMP%d" % i) for i in range(4)]
        MEAN = mk(F32, 1, TB, "MEAN")
        RSTD = mk(F32, 1, TB, "RSTD")
        VAR = mk(F32, 1, TB, "VAR")
        WS = [mk(BF16, 16, 128, "WS%d" % i) for i in range(NW)]
        VEC = mk(F32, 1, NV, "VEC")
        CSL = mk(F32, 1, TB, "CSL")
        CST = mk(F32, 1, 384, "CST")
        IDB = mk(BF16, 1, 128, "IDB")
        MSK = mk(BF16, 1, 128, "MSK")
        FLD = mk(BF16, 1, 128, "FLD")
        ONE = mk(BF16, 1, 128, "ONE")
        SSB = mk(F32, 1, 8, "SSB")

        def psb(b, c0=0, c1=512):
            return R(ps_t[:, b, c0:c1], [("ps", b)])

        def psb_bf(b):
            return R(ps_t[:, b, :].bitcast(BF16), [("ps", b)])

        def vcol(name, j=0):
            c = VOFF[name] + j
            return R(VEC.ap[:, 0, c:c + 1], VEC._keys(0, 1, c, c + 1))

        def mm(out, lhsT, rhs, start, stop, signal=None):
            S.op("pe", lambda e, o=out.ap, l=lhsT.ap, r=rhs.ap, a=start, b=stop: e.matmul(o, lhsT=l, rhs=r, start=a, stop=b),
                 reads=[lhsT, rhs], writes=[out], signal=(stop if signal is None else signal))

        def tr(out, in_, ident):
            S.op("pe", lambda e, o=out.ap, i=in_.ap, d=ident.ap: e.transpose(o, i, d), reads=[in_, ident], writes=[out])

        def act(out, in_, func, bias=None, scale=None, accum=None, extra_reads=()):
            kw = {}
            rd = [in_] + list(extra_reads)
            wr = [out]
            if bias is not None:
                kw["bias"] = bias.ap
                rd.append(bias)
            if scale is not None:
                if isinstance(scale, R):
                    kw["scale"] = scale.ap
                    rd.append(scale)
                else:
                    kw["scale"] = float(scale)
            if accum is not None:
                kw["accum_out"] = accum.ap
                wr.append(accum)
            S.op("act", lambda e, o=out.ap, i=in_.ap, f=func, kw=kw: e.activation(out=o, in_=i, func=f, **kw), reads=rd, writes=wr)

        def _sc(x, rd):
            if isinstance(x, R):
                rd.append(x)
                return x.ap
            return x

        def tsc(out, in0, s1, s2, op0, op1=None):
            rd = [in0]
            a1 = _sc(s1, rd)
            a2 = _sc(s2, rd) if s2 is not None else None
            if op1 is None:
                S.op("dve", lambda e, o=out.ap, i=in0.ap: e.tensor_scalar(out=o, in0=i, scalar1=a1, scalar2=None, op0=op0), reads=rd, writes=[out])
            else:
                S.op("dve", lambda e, o=out.ap, i=in0.ap: e.tensor_scalar(out=o, in0=i, scalar1=a1, scalar2=a2, op0=op0, op1=op1), reads=rd, writes=[out])

        def stt(out, in0, scalar, in1, op0, op1):
            rd = [in0, in1]
            a = _sc(scalar, rd)
            S.op("dve", lambda e, o=out.ap, i0=in0.ap, i1=in1.ap: e.scalar_tensor_tensor(out=o, in0=i0, scalar=a, in1=i1, op0=op0, op1=op1),
                 reads=rd, writes=[out])

        def tt(out, in0, in1, op):
            S.op("dve", lambda e, o=out.ap, i0=in0.ap, i1=in1.ap: e.tensor_tensor(out=o, in0=i0, in1=i1, op=op), reads=[in0, in1], writes=[out])

        def recip(out, in_):
            S.op("dve", lambda e, o=out.ap, i=in_.ap: e.reciprocal(out=o, in_=i), reads=[in_], writes=[out])

        def vcopy(out, in_):
            S.op("dve", lambda e, o=out.ap, i=in_.ap: e.tensor_copy(out=o, in_=i), reads=[in_], writes=[out])

        def vmemset(out, val):
            S.op("dve", lambda e, o=out.ap: e.memset(o, val), writes=[out])

        def dma(eng, out, in_, sem):
            S.op(eng, lambda e, o=out.ap, i=in_.ap: e.dma_start(out=o, in_=i), reads=[in_], writes=[out], dma_sem=sem)

        cp_flip = [0]

        def copy_any(out, in_):
            cp_flip[0] ^= 1
            if cp_flip[0]:
                act(out, in_, AF.Copy)
            else:
                vcopy(out, in_)

        wk = {"issued": 0, "k": 0}
        total_blocks = NB * len(tiles)

        def w_issue():
            k = wk["issued"]
            if k >= total_blocks:
                return
            wk["issued"] += 1
            s = k % NW
            dst = R(WS[s].ap.rearrange("p a b -> p (a b)"), WS[s]._keys(0, 16, 0, 128))
            dma("pool", dst, R(wst[k % NB]), "w%d" % s)

        def w_next(desc):
            k = wk["k"]
            assert ORDER[k % NB] == desc, (ORDER[k % NB], desc)
            wk["k"] += 1
            return WS[k % NW]

        def wv16(ws, kc):
            return ws.v(kc)

        def wv4(ws, kc, c0, c1):
            ap = ws.ap.rearrange("p a b -> p (a b)")[:, kc * 512 + c0: kc * 512 + c1]
            return R(ap, ws._keys(0, 16, 0, 128))

        psr = {"i": 0, "lo": 0, "od": 0}

        def ps_next(n=1):
            b = psr["i"]
            if b + n > 8:
                b = 0
            psr["i"] = (b + n) % 8
            return list(range(b, b + n))

        def ps_lo():
            b = psr["lo"]
            psr["lo"] = (b + 1) % 4
            return b

        def ps_od():
            b = 4 + 2 * psr["od"]
            psr["od"] ^= 1
            return b, b + 1

        dma("sp", VEC.v(0), R(vecs_d), "c0")
        dma("sp", CST.v(0), R(consts_d), "c1")
        for i in range(NW):
            w_issue()
        ident_f = CST.v(0, 0, 128)
        act(IDB.v(0), CST.v(0, 0, 128), AF.Copy)
        act(MSK.v(0), CST.v(0, 128, 256), AF.Copy)
        act(FLD.v(0), CST.v(0, 256, 384), AF.Copy)
        vmemset(ONE.v(0), 1.0)
        vmemset(CKV.v3(0, 4), 0.0)
        vmemset(KR.v(0), 0.0)
        vmemset(CB.v3(0, NCH, 0, 32), 0.0)
        identb, maskb, foldb, ones = IDB.v(0), MSK.v(0), FLD.v(0), ONE.v(0)
        eps_c = vcol("eps")

        xs_use = [0]

        def sumsq_rstd(src_fn, nch, slabs, dim, tmp_i=0):
            for (c0, n) in slabs:
                (b,) = ps_next(1)
                for ch in range(nch):
                    sq = TMPS[(tmp_i + ch) % 2]
                    sqv = R(sq.ap[:, 0, 0:n].bitcast(BF16)[:, 0:n], sq._keys(0, 1, 0, n))
                    act(sqv, src_fn(ch, c0, n), AF.Square)
                    mm(psb(b, 0, n), ones, sqv, ch == 0, ch == nch - 1, signal=True)
                act(RSTD.v(0, c0, c0 + n), psb(b, 0, n), AF.Sqrt, bias=eps_c, scale=1.0 / dim)
                recip(RSTD.v(0, c0, c0 + n), RSTD.v(0, c0, c0 + n))

        def gemm16(ws, rhs_fn, slabs, banks):
            for kc in range(16):
                for si, (c0, n) in enumerate(slabs):
                    mm(psb(banks[si], 0, n), wv16(ws, kc), rhs_fn(kc, c0, n), kc == 0, kc == 15)

        for ti, (t0, T) in enumerate(tiles):
            slabs = [(c, min(SL, T - c)) for c in range(0, T, SL)]
            nblk = (T + 127) // 128
            NS = len(slabs)

            dma("sp", GMX.v(0), R(gmix_d.partition_broadcast(128)), "c2")
            dma("sp", CSL.v(0, 0, T), R(cs_d[:, t0:t0 + T]), "c3")
            for j in range(nblk):
                rows = min(128, T - j * 128)
                xs = XS[xs_use[0] % 2]
                xsem = "x%d" % (xs_use[0] % 2)
                xs_use[0] += 1
                if rows < 128:
                    vmemset(xs.v(0), 0.0)
                dma("sp", xs.v(0, 0, D, 0, rows), R(h0[t0 + j * 128: t0 + j * 128 + rows, :]), xsem)
                ss = SSB.v(0, 0, 1)
                vmemset(ss, 0.0)
                act(JUNK.v(0), xs.v(0), AF.Square, accum=ss)
                act(SSB.v(0, 1, 2), ss, AF.Sqrt, bias=eps_c, scale=1.0 / D)
                recip(SSB.v(0, 2, 3), SSB.v(0, 1, 2))
                stt(UTM.v(0), xs.v(0), SSB.v(0, 2, 3), GMX.v(0), ALU.mult, ALU.mult)
                for half in range(2):
                    (b,) = ps_next(1)
                    pb = psb_bf(b)
                    for c in range(8):
                        ch = half * 8 + c
                        tr(R(pb.ap[:, c * 128:(c + 1) * 128], pb.keys), UTM.v(0, ch * 128, (ch + 1) * 128), identb)
                    src = R(pb.ap.rearrange("p (c w) -> p c w", c=8), pb.keys)
                    copy_any(B1.v3(half * 8, half * 8 + 8, j * 128, (j + 1) * 128), src)

            def u_rhs(kc, c0, n):
                return B1.v(kc, c0, c0 + n)
            for name, dst, gname, col0 in (("cq", CQ, "g_q", 0), ("ckv", CKV, "g_kv", t0)):
                for j in range(4):
                    ws = w_next(("in", name, j))
                    banks = ps_next(NS)
                    gemm16(ws, u_rhs, slabs, banks)
                    w_issue()
                    for si, (c0, n) in enumerate(slabs):
                        copy_any(dst.v(j, col0 + c0, col0 + c0 + n), psb(banks[si], 0, n))
                sumsq_rstd(lambda ch, c0, n, dst=dst, col0=col0: dst.v(ch, col0 + c0, col0 + c0 + n), 4, slabs, 512.0)
                for j in range(4):
                    for (c0, n) in slabs:
                        d_ = dst.v(j, col0 + c0, col0 + c0 + n)
                        stt(d_, d_, vcol(gname, j), RSTD.v(0, c0, c0 + n), ALU.mult, ALU.mult)
            ws = w_next(("in", "kr", 0))
            banks = ps_next(NS)
            gemm16(ws, u_rhs, slabs, banks)
            w_issue()
            for si, (c0, n) in enumerate(slabs):
                tmp = TMPS[2 + si % 2]
                tv = R(tmp.ap[:, 0, 0:n].bitcast(BF16)[:, 0:n], tmp._keys(0, 1, 0, n))
                tt(tv, psb(banks[si], 0, n), CSL.v(0, c0, c0 + n), ALU.mult)
                (b,) = ps_next(1)
                mm(psb(b, 0, n), foldb, tv, True, True)
                copy_any(KR.v(0, t0 + c0, t0 + c0 + n), psb(b, 0, n))

            for j in range(16):
                wa = w_next(("in", "za", j))
                ba = ps_next(NS)
                gemm16(wa, u_rhs, slabs, ba)
                w_issue()
                wb = w_next(("in", "zb", j))
                bb = ps_next(NS)
                gemm16(wb, u_rhs, slabs, bb)
                w_issue()
                for si, (c0, n) in enumerate(slabs):
                    sg = TMPS[si % 2].v(0, 0, n)
                    act(sg, psb(bb[si], 0, n), AF.Sigmoid, bias=vcol("b_zb", j))
                    stt(CB.v(j, 32 + c0, 32 + c0 + n), psb(ba[si], 0, n), vcol("b_za", j), sg, ALU.add, ALU.mult)

            for ch in range(16):
                acc = CACC[ch % 2].v(0, 0, T)
                tsc(acc, CB.v(ch, 2, 2 + T), vcol("w_dw", ch), vcol("b_dw", ch), ALU.mult, ALU.add)
                for jj in range(1, 31):
                    stt(acc, CB.v(ch, 2 + jj, 2 + jj + T), vcol("w_dw", jj * 16 + ch), acc, ALU.mult, ALU.add)
                act(B3.v(ch, 0, T), acc, AF.Copy)
            vcopy(CB.v3(0, NCH, 0, 32), CB.v3(0, NCH, T, T + 32))

            kend = t0 + T
            nkt = (kend + 127) // 128
            for g in range(4):
                wuv = w_next(("uv", g))
                for kt in range(nkt):
                    b = ps_lo()
                    for kc in range(4):
                        mm(psb(b), CKV.v(kc, kt * 128, kt * 128 + 128), wv4(wuv, kc, 0, 512), kc == 0, kc == 3)
                    copy_any(VG.v(kt), psb(b))
                w_issue()
                wuk = w_next(("uk", g))
                wq = [w_next(("uq", 2 * g)), w_next(("uq", 2 * g + 1))]
                for hh in range(4):
                    h = g * 4 + hh
                    kcols = nkt * 128
                    for k0 in range(0, kcols, 512):
                        kn = min(512, kcols - k0)
                        b = ps_lo()
                        for kc in range(4):
                            mm(psb(b, 0, kn), wv4(wuk, kc, hh * 128, hh * 128 + 128), CKV.v(kc, k0, k0 + kn), kc == 0, kc == 3)
                        copy_any(KH.v(0, k0, k0 + kn), psb(b, 0, kn))
                    wqs = wq[hh // 2]
                    qo = (hh % 2) * 256
                    for (c0, n) in slabs:
                        b = ps_lo()
                        for kc in range(4):
                            mm(psb(b, 0, n), wv4(wqs, kc, qo, qo + 128), CQ.v(kc, c0, c0 + n), kc == 0, kc == 3)
                        act(QN.v(0, c0, c0 + n), psb(b, 0, n), AF.Copy)
                        b2 = ps_lo()
                        for kc in range(4):
                            mm(psb(b2, 0, n), wv4(wqs, kc, qo + 128, qo + 256), CQ.v(kc, c0, c0 + n), kc == 0, kc == 3)
                        tt(QR.v(0, c0, c0 + n), psb(b2, 0, n), CSL.v(0, c0, c0 + n), ALU.mult)
                    for (c0, n) in slabs:
                        q0 = t0 + c0
                        nk = (q0 + n - 1) // 128 + 1
                        bo, bd = ps_od()
                        blocks = []
                        for kt in range(nk):
                            qs = max(q0, 128 * kt)
                            blocks.append((kt, qs - q0, n - (qs - q0), 128 * kt >= q0))

                        def qk(i):
                            kt, cs_, N, diag = blocks[i]
                            b = ps_lo()
                            mm(psb(b, 0, N), KH.v(0, kt * 128, kt * 128 + 128), QN.v(0, c0 + cs_, c0 + cs_ + N), True, False)
                            mm(psb(b, 0, N), KR.v(0, kt * 128, kt * 128 + 128), QR.v(0, c0 + cs_, c0 + cs_ + N), False, not diag)
                            if diag:
                                w_ = min(128, N)
                                mm(psb(b, 0, w_), identb, MSK.v(0, 0, w_), False, True)
                            return b
                        bs = qk(0)
                        for i, (kt, cs_, N, diag) in enumerate(blocks):
                            bnext = qk(i + 1) if i + 1 < len(blocks) else None
                            pt = PT[i % 3].v(0, 0, N)
                            act(pt, psb(bs, 0, N), AF.Exp, scale=SCALE)
                            mm(psb(bo, cs_, cs_ + N), VG.v(kt, hh * 128, hh * 128 + 128), pt, i == 0, i == len(blocks) - 1, signal=True)
                            mm(psb(bd, cs_, cs_ + N), ones, pt, i == 0, i == len(blocks) - 1, signal=True)
                            bs = bnext
                        recip(RDEN.v(0, 0, n), psb(bd, 0, n))
                        tt(B4.v(h, c0, c0 + n), psb(bo, 0, n), RDEN.v(0, 0, n), ALU.mult)
                w_issue()
                w_issue()
                w_issue()

            for (c0, n) in slabs:
                (bm,) = ps_next(1)
                (bq,) = ps_next(1)
                for ch in range(16):
                    sq = TMPS[ch % 2]
                    sqv = R(sq.ap[:, 0, 0:n].bitcast(BF16)[:, 0:n], sq._keys(0, 1, 0, n))
                    act(sqv, B3.v(ch, c0, c0 + n), AF.Square)
                    mm(psb(bm, 0, n), ones, B3.v(ch, c0, c0 + n), ch == 0, ch == 15, signal=True)
                    mm(psb(bq, 0, n), ones, sqv, ch == 0, ch == 15, signal=True)
                mean = MEAN.v(0, c0, c0 + n)
                var = VAR.v(0, c0, c0 + n)
                rstd = RSTD.v(0, c0, c0 + n)
                tsc(mean, psb(bm, 0, n), 1.0 / D, None, ALU.mult)
                tt(var, mean, mean, ALU.mult)
                stt(var, psb(bq, 0, n), 1.0 / D, var, ALU.mult, ALU.subtract)
                act(rstd, var, AF.Sqrt, bias=eps_c, scale=1.0)
                recip(rstd, rstd)
                for ch in range(16):
                    t1 = TMPS[2 + ch % 2].v(0, 0, n)
                    tt(t1, B3.v(ch, c0, c0 + n), mean, ALU.subtract)
                    stt(t1, t1, vcol("g_ln", ch), rstd, ALU.mult, ALU.mult)
                    act(CB.v(ch, 32 + c0, 32 + c0 + n), t1, AF.Silu, bias=vcol("b_ln", ch))

            for (wname, gname, bname, src_fn, first) in (
                    ("ao", "ga", "b_ga", lambda kc, c0, n: B4.v(kc, c0, c0 + n), True),
                    ("co", "gc", "b_gc", lambda kc, c0, n: CB.v(kc, 32 + c0, 32 + c0 + n), False)):
                for j in range(16):
                    wa = w_next((wname, j))
                    ba = ps_next(NS)
                    gemm16(wa, src_fn, slabs, ba)
                    w_issue()
                    wg = w_next(("in", gname, j))
                    bg = ps_next(NS)
                    gemm16(wg, u_rhs, slabs, bg)
                    w_issue()
                    for si, (c0, n) in enumerate(slabs):
                        sg = TMPS[si % 2].v(0, 0, n)
                        act(sg, psb(bg[si], 0, n), AF.Sigmoid, bias=vcol(bname, j))
                        if first:
                            tt(B3.v(j, c0, c0 + n), psb(ba[si], 0, n), sg, ALU.mult)
                        else:
                            t1 = TMPS[2 + si % 2].v(0, 0, n)
                            stt(t1, psb(ba[si], 0, n), vcol("b_co", j), sg, ALU.add, ALU.mult)
                            tt(B3.v(j, c0, c0 + n), B3.v(j, c0, c0 + n), t1, ALU.add)

            for j in range(nblk):
                rows = min(128, T - j * 128)
                xs = XS[xs_use[0] % 2]
                xsem = "x%d" % (xs_use[0] % 2)
                xs_use[0] += 1
                if rows < 128:
                    vmemset(xs.v(0), 0.0)
                dma("sp", xs.v(0, 0, D, 0, rows), R(h0[t0 + j * 128: t0 + j * 128 + rows, :]), xsem)
                for q4 in range(4):
                    (b,) = ps_next(1)
                    for c in range(4):
                        ch = q4 * 4 + c
                        tr(psb(b, c * 128, (c + 1) * 128), xs.v(0, ch * 128, (ch + 1) * 128), ident_f)
                    src = R(ps_t[:, b, :].rearrange("p (c w) -> p c w", c=4), [("ps", b)])
                    copy_any(H.v3(q4 * 4, q4 * 4 + 4, j * 128, (j + 1) * 128), src)
            for j in range(16):
                ws = w_next(("wo", j))
                banks = ps_next(NS)
                gemm16(ws, lambda kc, c0, n: B3.v(kc, c0, c0 + n), slabs, banks)
                w_issue()
                for si, (c0, n) in enumerate(slabs):
                    tt(H.v(j, c0, c0 + n), psb(banks[si], 0, n), H.v(j, c0, c0 + n), ALU.add)

            sumsq_rstd(lambda ch, c0, n: H.v(ch, c0, c0 + n), 16, slabs, float(D))
            for ch in range(16):
                for (c0, n) in slabs:
                    stt(B3.v(ch, c0, c0 + n), H.v(ch, c0, c0 + n), vcol("g_ffn", ch), RSTD.v(0, c0, c0 + n), ALU.mult, ALU.mult)

            def hn_rhs(kc, c0, n):
                return B3.v(kc, c0, c0 + n)
            for g in range(NFG):
                ag = ACTG[g % 2]
                for f in range(4):
                    wg = w_next(("fg", g * 4 + f))
                    bg = ps_next(NS)
                    gemm16(wg, hn_rhs, slabs, bg)
                    w_issue()
                    wu = w_next(("fu", g * 4 + f))
                    bu = ps_next(NS)
                    gemm16(wu, hn_rhs, slabs, bu)
                    w_issue()
                    for si, (c0, n) in enumerate(slabs):
                        sg = TMPS[si % 2].v(0, 0, n)
                        act(sg, psb(bg[si], 0, n), AF.Silu)
                        tt(ag.v(f, c0, c0 + n), psb(bu[si], 0, n), sg, ALU.mult)
                for cb in range(4):
                    wd = w_next(("fd", g, cb))
                    for o4 in range(4):
                        oc = cb * 4 + o4
                        banks = ps_next(NS)
                        for kc in range(4):
                            for si, (c0, n) in enumerate(slabs):
                                mm(psb(banks[si], 0, n), wv4(wd, kc, o4 * 128, o4 * 128 + 128), ag.v(kc, c0, c0 + n), kc == 0, kc == 3)
                        for si, (c0, n) in enumerate(slabs):
                            tt(H.v(oc, c0, c0 + n), psb(banks[si], 0, n), H.v(oc, c0, c0 + n), ALU.add)
                    w_issue()

            sumsq_rstd(lambda ch, c0, n: H.v(ch, c0, c0 + n), 16, slabs, float(D))
            for ch in range(16):
                for (c0, n) in slabs:
                    stt(H.v(ch, c0, c0 + n), H.v(ch, c0, c0 + n), vcol("g_fin", ch), RSTD.v(0, c0, c0 + n), ALU.mult, ALU.mult)
            for j in range(nblk):
                rows = min(128, T - j * 128)
                xs = XS[xs_use[0] % 2]
                xsem = "x%d" % (xs_use[0] % 2)
                xs_use[0] += 1
                for q4 in range(4):
                    (b,) = ps_next(1)
                    for c in range(4):
                        ch = q4 * 4 + c
                        tr(psb(b, c * 128, (c + 1) * 128), H.v(ch, j * 128, (j + 1) * 128), ident_f)
                    copy_any(xs.v(0, q4 * 512, (q4 + 1) * 512), psb(b))
                tok0 = t0 + j * 128
                p0 = max(0, NMETA - tok0)
                if rows > p0:
                    dma("sp", R(out_d[tok0 + p0 - NMETA: tok0 + rows - NMETA, :]), xs.v(0, 0, D, p0, rows), xsem)

        assert wk["k"] == total_blocks and wk["issued"] == total_blocks, (wk, total_blocks)

        dma_sems = sorted(S.dma_count.keys())
        sem_names = list(S.ops.keys()) + dma_sems
        semh = {n: es.enter_context(nc.semaphore("s_" + n)) for n in sem_names}
        for k, v in S.maxwait.items():
            lim = S.count[k] if k in S.count else S.dma_count[k]
            assert v <= lim, ("wait beyond final count", k, v, lim)
        block = es.enter_context(nc.Block())

        def run(e, name):
            for fn, waits, signal, dsem in S.ops[name]:
                for (k, v) in waits:
                    e.wait_ge(semh[k], v)
                ins = fn(e)
                if signal:
                    ins.then_inc(semh[name], 1)
                if dsem is not None:
                    ins.then_inc(semh[dsem], 16)
            if name == "sp":
                for k in dma_sems:
                    if k.startswith("x") or k.startswith("c"):
                        e.wait_ge(semh[k], S.dma_count[k])

        @block.tensor
        def _(e):
            run(e, "pe")

        @block.scalar
        def _(e):
            run(e, "act")

        @block.vector
        def _(e):
            run(e, "dve")

        @block.gpsimd
        def _(e):
            run(e, "pool")

        @block.sync
        def _(e):
            run(e, "sp")
    stats = {n: len(v) for n, v in S.ops.items()}
    return nc, stats


_CACHE = {}


def kernel(x, meta_tokens, g_mix, w_in, b_glu, b_gate, w_dw, b_dw, g_conv_ln, b_conv_ln, w_conv_out, b_conv_out,
           g_q_lora, w_uq, g_kv_lora, w_uk, w_uv, w_attn_out, w_out, g_ffn, w_ffn_gate, w_ffn_up, w_ffn_down, g_final):
    f = lambda a: np.asarray(a, np.float32)
    x = f(x)
    meta = f(meta_tokens)
    wst = pack_weights(f(w_in)[0], f(w_conv_out)[0], f(w_uq)[0], f(w_uk)[0], f(w_uv)[0], f(w_attn_out)[0], f(w_out)[0],
                       f(w_ffn_gate)[0], f(w_ffn_up)[0], f(w_ffn_down)[0])
    vecs = pack_vecs(f(b_glu)[0], f(b_gate)[0], f(b_dw)[0], f(g_conv_ln)[0], f(b_conv_ln)[0], f(b_conv_out)[0],
                     f(g_q_lora)[0], f(g_kv_lora)[0], f(g_ffn)[0], f(g_final), f(w_dw)[0])
    consts, cs = const_tables()
    gmix = np.ascontiguousarray(f(g_mix)[0])
    if "nc" not in _CACHE:
        _CACHE["nc"] = build_program()[0]
    nc = _CACHE["nc"]
    nb = x.shape[0]
    in_maps = []
    for b in range(nb):
        h0 = np.ascontiguousarray(np.concatenate([meta, x[b]], axis=0))
        in_maps.append({"h0": h0, "wst": wst, "vecs": vecs, "consts": consts, "cs": cs, "gmix": gmix})
    res = run_bass_kernel_spmd(nc, in_maps, core_ids=list(range(nb)))
    return np.stack([np.asarray(r["out"], np.float32) for r in res.results], axis=0)
```

```python
import numpy as np
import concourse.bass as bass
import concourse.mybir as mybir
from concourse.bass_utils import run_bass_kernel_spmd

F32, BF16 = mybir.dt.float32, mybir.dt.bfloat16
AF = mybir.ActivationFunctionType
ALU = mybir.AluOpType

D = 2048
NCH = 16
NMETA = 16
SEQ = 2048
L = NMETA + SEQ
LP = 2176
TILES = [(0, 768), (768, 768), (1536, 528)]
SL = 384
TB = 768
DFF = 5632
NFG = 11
NH = 16
SCALE = float(192 ** -0.5)
EPS = 1e-6
NW = 4
CONV_DRAIN_B2 = 14
GR = 256


def stream_order():
    o = []
    for j in range(4):
        o.append(("in", "cq", j))
    for j in range(4):
        o.append(("in", "ckv", j))
    o.append(("in", "kr", 0))
    for j in range(16):
        o.append(("in", "za", j))
        o.append(("in", "zb", j))
    for g in range(4):
        o.append(("uv", g))
        o.append(("uk", g))
        o.append(("uq", 2 * g))
        o.append(("uq", 2 * g + 1))
    for j in range(16):
        o.append(("ao", j))
        o.append(("in", "ga", j))
    for j in range(16):
        o.append(("co", j))
        o.append(("in", "gc", j))
    for j in range(16):
        o.append(("wo", j))
    for g in range(NFG):
        for f in range(4):
            o.append(("fg", g * 4 + f))
            o.append(("fu", g * 4 + f))
        for cb in range(4):
            o.append(("fd", g, cb))
    return o


ORDER = stream_order()
NB = len(ORDER)

VOFF = {}
_c = 0
for _n, _w in [("b_za", 16), ("b_zb", 16), ("b_gc", 16), ("b_ga", 16), ("b_dw", 16), ("g_ln", 16), ("b_ln", 16),
               ("b_co", 16), ("g_q", 4), ("g_kv", 4), ("g_ffn", 16), ("g_fin", 16), ("eps", 1), ("w_dw", 31 * 16)]:
    VOFF[_n] = _c
    _c += _w
NV = _c


def _pack_k2048(W, cols):
    return np.ascontiguousarray(W[:, cols].reshape(16, 128, 128).transpose(1, 0, 2).reshape(128, 2048))


def _pack_k512(W, cols):
    return np.ascontiguousarray(W[:, cols].reshape(4, 128, 512).transpose(1, 0, 2).reshape(128, 2048))


def pack_weights(w_in, w_conv_out, w_uq, w_uk, w_uv, w_attn_out, w_out, w_ffn_gate, w_ffn_up, w_ffn_down):
    wst = np.empty((NB, 128, 2048), np.float32)
    ar = np.arange
    base = {"za": 0, "zb": 2048, "cq": 4096, "ckv": 4608, "gc": 5184, "ga": 7232}
    for i, d in enumerate(ORDER):
        t = d[0]
        if t == "in":
            if d[1] == "kr":
                r = 5120 + ar(64)
                sw = 5120 + np.concatenate([ar(32, 64), ar(0, 32)])
                cols = np.concatenate([r, sw])
            else:
                cols = base[d[1]] + d[2] * 128 + ar(128)
            wst[i] = _pack_k2048(w_in, cols)
        elif t == "co":
            wst[i] = _pack_k2048(w_conv_out, d[1] * 128 + ar(128))
        elif t == "ao":
            wst[i] = _pack_k2048(w_attn_out, d[1] * 128 + ar(128))
        elif t == "wo":
            wst[i] = _pack_k2048(w_out, d[1] * 128 + ar(128))
        elif t == "fg":
            wst[i] = _pack_k2048(w_ffn_gate, d[1] * 128 + ar(128))
        elif t == "fu":
            wst[i] = _pack_k2048(w_ffn_up, d[1] * 128 + ar(128))
        elif t == "fd":
            g, cb = d[1], d[2]
            blk = w_ffn_down[g * 512:(g + 1) * 512, cb * 512:(cb + 1) * 512]
            wst[i] = blk.reshape(4, 128, 512).transpose(1, 0, 2).reshape(128, 2048)
        elif t == "uv":
            wst[i] = _pack_k512(w_uv, d[1] * 512 + ar(512))
        elif t == "uk":
            wst[i] = _pack_k512(w_uk, d[1] * 512 + ar(512))
        elif t == "uq":
            cols = []
            for h in (2 * d[1], 2 * d[1] + 1):
                b = h * 192
                cols += [b + ar(128), b + 128 + ar(64), b + 128 + ar(32, 64), b + 128 + ar(0, 32)]
            wst[i] = _pack_k512(w_uq, np.concatenate(cols))
        else:
            raise ValueError(t)
    return wst


def _col(v):
    return np.ascontiguousarray(np.asarray(v, np.float32).reshape(-1, 128).T)


def pack_vecs(b_glu, b_gate, b_dw, g_conv_ln, b_conv_ln, b_conv_out, g_q_lora, g_kv_lora, g_ffn, g_final, w_dw):
    v = np.zeros((128, NV), np.float32)

    def put(name, arr):
        c = _col(arr)
        v[:, VOFF[name]:VOFF[name] + c.shape[1]] = c
    put("b_za", b_glu[:2048])
    put("b_zb", b_glu[2048:])
    put("b_gc", b_gate[:2048])
    put("b_ga", b_gate[2048:])
    put("b_dw", b_dw)
    put("g_ln", g_conv_ln)
    put("b_ln", b_conv_ln)
    put("b_co", b_conv_out)
    put("g_q", g_q_lora)
    put("g_kv", g_kv_lora)
    put("g_ffn", g_ffn)
    put("g_fin", g_final)
    v[:, VOFF["eps"]] = EPS
    for j in range(31):
        c = _col(w_dw[j])
        v[:, VOFF["w_dw"] + j * 16: VOFF["w_dw"] + (j + 1) * 16] = c
    return v


def const_tables():
    p = np.arange(128)
    ident = np.eye(128, dtype=np.float32)
    maskneg = np.where(p[:, None] <= p[None, :], 0.0, -30000.0).astype(np.float32)
    fold = (p[:, None] % 64 == p[None, :] % 64).astype(np.float32)
    consts = np.concatenate([ident, maskneg, fold], axis=1)
    pos = np.arange(LP, dtype=np.float32)
    inv_freq = (np.float32(10000.0) ** (-np.arange(0, 64, 2, dtype=np.float32) / np.float32(64))).astype(np.float32)
    ang = (pos[None, :] * inv_freq[:, None]).astype(np.float32)
    cos, sin = np.cos(ang).astype(np.float32), np.sin(ang).astype(np.float32)
    cs = np.concatenate([cos, cos, -sin, sin], axis=0).astype(np.float32)
    return np.ascontiguousarray(consts), np.ascontiguousarray(cs)


class R:
    __slots__ = ("ap", "keys")

    def __init__(self, ap, keys=()):
        self.ap = ap
        self.keys = frozenset(keys)


class Sched:
    def __init__(self):
        self.ops = {n: [] for n in ("pe", "act", "dve", "pool", "sp")}
        self.count = {n: 0 for n in self.ops}
        self.waited = {n: {} for n in self.ops}
        self.last_w = {}
        self.readers = {}
        self.dma_count = {}
        self.maxwait = {}

    def op(self, eng, fn, reads=(), writes=(), signal=True, dma_sem=None):
        deps = {}

        def need(ev):
            if ev is not None and deps.get(ev[0], 0) < ev[1]:
                deps[ev[0]] = ev[1]
        rk = set()
        for r in reads:
            rk |= r.keys
        wk = set()
        for w in writes:
            wk |= w.keys
        for k in rk:
            need(self.last_w.get(k))
        for k in wk:
            need(self.last_w.get(k))
            rd = self.readers.get(k)
            if rd:
                for s, v in rd.items():
                    need((s, v))
        if dma_sem is not None:
            self.dma_count[dma_sem] = self.dma_count.get(dma_sem, 0) + 16
            ev = (dma_sem, self.dma_count[dma_sem])
            signal = False
        elif signal:
            self.count[eng] += 1
            ev = (eng, self.count[eng])
        else:
            ev = (eng, self.count[eng] + 1)
        waits = []
        wd = self.waited[eng]
        for k, v in deps.items():
            if k == eng:
                continue
            if wd.get(k, 0) >= v:
                continue
            wd[k] = v
            waits.append((k, v))
            if self.maxwait.get(k, 0) < v:
                self.maxwait[k] = v
        self.ops[eng].append((fn, waits, signal, dma_sem))
        for k in rk:
            d = self.readers.setdefault(k, {})
            if d.get(ev[0], 0) < ev[1]:
                d[ev[0]] = ev[1]
        for k in wk:
            self.last_w[k] = ev
            self.readers[k] = {}
        return ev


class Buf:
    def __init__(self, arena, off, dtype, n0, n1, tag):
        self.off, self.dtype, self.n0, self.n1, self.tag = off, dtype, n0, n1, tag
        self.es = 4 if dtype == F32 else 2
        nb = n0 * n1 * self.es
        a = arena[:, off // 2: (off + nb) // 2]
        if dtype == F32:
            a = a.bitcast(F32)
        self.ap = a.rearrange("p (a b) -> p a b", a=n0)

    def _keys(self, i0, i1, c0, c1):
        ks = set()
        for i in range(i0, i1):
            lo = self.off + (i * self.n1 + c0) * self.es
            hi = self.off + (i * self.n1 + c1) * self.es
            ks.update(("sb", g) for g in range(lo // GR, (hi - 1) // GR + 1))
        return ks

    def v(self, i, c0=0, c1=None, p0=0, p1=128):
        c1 = self.n1 if c1 is None else c1
        return R(self.ap[p0:p1, i, c0:c1], self._keys(i, i + 1, c0, c1))

    def v3(self, i0, i1, c0=0, c1=None):
        c1 = self.n1 if c1 is None else c1
        return R(self.ap[:, i0:i1, c0:c1], self._keys(i0, i1, c0, c1))


def build_program(ntiles=3, dbg=None):
    nc = bass.Bass("TRN2", target_bir_lowering=False)
    h0 = nc.dram_tensor("h0", [L, D], F32, kind="ExternalInput").ap()
    wst = nc.dram_tensor("wst", [NB, 128, 2048], F32, kind="ExternalInput").ap()
    vecs_d = nc.dram_tensor("vecs", [128, NV], F32, kind="ExternalInput").ap()
    consts_d = nc.dram_tensor("consts", [128, 384], F32, kind="ExternalInput").ap()
    cs_d = nc.dram_tensor("cs", [128, LP], F32, kind="ExternalInput").ap()
    gmix_d = nc.dram_tensor("gmix", [D], F32, kind="ExternalInput").ap()
    out_d = nc.dram_tensor("out", [SEQ, D], F32, kind="ExternalOutput").ap()
    dbg_d = {}
    if dbg:
        for name, shape in dbg.items():
            dbg_d[name] = nc.dram_tensor("dbg_" + name, list(shape), F32, kind="ExternalOutput").ap()

    tiles = TILES[:ntiles]
    S = Sched()
    ARENA_BYTES = 207 * 1024 + 512
    import contextlib
    with contextlib.ExitStack() as es:
        arena = es.enter_context(nc.sbuf_tensor("arena", [128, ARENA_BYTES // 2], BF16))
        ps_t = es.enter_context(nc.psum_tensor("ps", [128, 8, 512], F32))
        cur = [0]

        def alloc(nbytes):
            o = cur[0]
            cur[0] = (o + nbytes + GR - 1) // GR * GR
            assert cur[0] <= ARENA_BYTES, ("SBUF overflow", cur[0])
            return o

        def mk(dtype, n0, n1, tag, off=None):
            es_ = 4 if dtype == F32 else 2
            if off is None:
                off = alloc(n0 * n1 * es_)
            return Buf(arena, off, dtype, n0, n1, tag)

        o_bb = alloc(2 * NCH * TB * 2)
        B1 = mk(BF16, NCH, TB, "B1", o_bb)
        B4 = mk(BF16, NCH, TB, "B4", o_bb + NCH * TB * 2)
        H = mk(F32, NCH, TB, "H", o_bb)
        CBW = 32 + TB
        CB = mk(BF16, NCH, CBW, "CB")
        B3 = mk(BF16, NCH, TB, "B3")
        CQ = mk(BF16, 4, TB, "CQ")
        CKV = mk(BF16, 4, LP, "CKV")
        KR = mk(BF16, 1, LP, "KR")
        o_sa = alloc(32768)
        XS = [mk(F32, 1, D, "XS%d" % i, o_sa + i * 8192) for i in range(2)]
        UTM = mk(BF16, 1, D, "UTM", o_sa + 16384)
        JUNK = mk(BF16, 1, D, "JUNK", o_sa + 20480)
        GMX = mk(F32, 1, D, "GMX", o_sa + 24576)
        ACTG = [mk(BF16, 4, TB, "ACTG%d" % i, o_sa + 16384 + i * 4 * TB * 2) for i in range(2)]
        VG = mk(BF16, 17, 512, "VG", o_sa)
        KH = mk(BF16, 1, LP, "KH", o_sa + 17408)
        QN = mk(BF16, 1, TB, "QN", o_sa + 17408 + 4352)
        QR = mk(BF16, 1, TB, "QR", o_sa + 17408 + 4352 + 1536)
        PT = [mk(BF16, 1, SL, "PT%d" % i, o_sa + 17408 + 4352 + 3072 + i * 768) for i in range(3)]
        RDEN = mk(F32, 1, SL, "RDEN", o_sa + 17408 + 4352 + 3072 + 2304)
        assert 17408 + 4352 + 3072 + 2304 + 1536 <= 32768
        CACC = [mk(F32, 1, TB, "CACC%d" % i) for i in range(2)]
        TMPS = [mk(F32, 1, SL, "TMP%d" % i) for i in range(4)]
        MEAN = mk(F32, 1, TB, "MEAN")
        RSTD = mk(F32, 1, TB, "RSTD")
        VAR = mk(F32, 1, TB, "VAR")
        WS = [mk(BF16, 16, 128, "WS%d" % i) for i in range(NW)]
        VEC = mk(F32, 1, NV, "VEC")
        CSL = mk(F32, 1, TB, "CSL")
        CST = mk(F32, 1, 384, "CST")
        IDB = mk(BF16, 1, 128, "IDB")
        MSK = mk(BF16, 1, 128, "MSK")
        FLD = mk(BF16, 1, 128, "FLD")
        ONE = mk(BF16, 1, 128, "ONE")
        SSB = mk(F32, 1, 8, "SSB")

        def psb(b, c0=0, c1=512):
            return R(ps_t[:, b, c0:c1], [("ps", b)])

        def psb_bf(b):
            return R(ps_t[:, b, :].bitcast(BF16), [("ps", b)])

        def vcol(name, j=0):
            c = VOFF[name] + j
            return R(VEC.ap[:, 0, c:c + 1], VEC._keys(0, 1, c, c + 1))

        def mm(out, lhsT, rhs, start, stop, signal=None):
            S.op("pe", lambda e, o=out.ap, l=lhsT.ap, r=rhs.ap, a=start, b=stop: e.matmul(o, lhsT=l, rhs=r, start=a, stop=b),
                 reads=[lhsT, rhs], writes=[out], signal=(stop if signal is None else signal))

        def tr(out, in_, ident):
            S.op("pe", lambda e, o=out.ap, i=in_.ap, d=ident.ap: e.transpose(o, i, d), reads=[in_, ident], writes=[out])

        def act(out, in_, func, bias=None, scale=None, accum=None, extra_reads=()):
            kw = {}
            rd = [in_] + list(extra_reads)
            wr = [out]
            if bias is not None:
                kw["bias"] = bias.ap
                rd.append(bias)
            if scale is not None:
                if isinstance(scale, R):
                    kw["scale"] = scale.ap
                    rd.append(scale)
                else:
                    kw["scale"] = float(scale)
            if accum is not None:
                kw["accum_out"] = accum.ap
                wr.append(accum)
            S.op("act", lambda e, o=out.ap, i=in_.ap, f=func, kw=kw: e.activation(out=o, in_=i, func=f, **kw), reads=rd, writes=wr)

        def _sc(x, rd):
            if isinstance(x, R):
                rd.append(x)
                return x.ap
            return x

        def tsc(out, in0, s1, s2, op0, op1=None):
            rd = [in0]
            a1 = _sc(s1, rd)
            a2 = _sc(s2, rd) if s2 is not None else None
            if op1 is None:
                S.op("dve", lambda e, o=out.ap, i=in0.ap: e.tensor_scalar(out=o, in0=i, scalar1=a1, scalar2=None, op0=op0), reads=rd, writes=[out])
            else:
                S.op("dve", lambda e, o=out.ap, i=in0.ap: e.tensor_scalar(out=o, in0=i, scalar1=a1, scalar2=a2, op0=op0, op1=op1), reads=rd, writes=[out])

        def stt(out, in0, scalar, in1, op0, op1):
            rd = [in0, in1]
            a = _sc(scalar, rd)
            S.op("dve", lambda e, o=out.ap, i0=in0.ap, i1=in1.ap: e.scalar_tensor_tensor(out=o, in0=i0, scalar=a, in1=i1, op0=op0, op1=op1),
                 reads=rd, writes=[out])

        def tt(out, in0, in1, op):
            S.op("dve", lambda e, o=out.ap, i0=in0.ap, i1=in1.ap: e.tensor_tensor(out=o, in0=i0, in1=i1, op=op), reads=[in0, in1], writes=[out])

        def recip(out, in_):
            S.op("dve", lambda e, o=out.ap, i=in_.ap: e.reciprocal(out=o, in_=i), reads=[in_], writes=[out])

        def vcopy(out, in_):
            S.op("dve", lambda e, o=out.ap, i=in_.ap: e.tensor_copy(out=o, in_=i), reads=[in_], writes=[out])

        def vmemset(out, val):
            S.op("dve", lambda e, o=out.ap: e.memset(o, val), writes=[out])

        def dma(eng, out, in_, sem):
            S.op(eng, lambda e, o=out.ap, i=in_.ap: e.dma_start(out=o, in_=i), reads=[in_], writes=[out], dma_sem=sem)

        cp_flip = [0]

        def copy_any(out, in_):
            cp_flip[0] ^= 1
            if cp_flip[0]:
                act(out, in_, AF.Copy)
            else:
                vcopy(out, in_)

        import collections
        bgq = collections.deque()

        def drain(n=None):
            k = len(bgq) if n is None else min(n, len(bgq))
            for _ in range(k):
                bgq.popleft()()

        wk = {"issued": 0, "k": 0}
        total_blocks = NB * len(tiles)

        def w_issue():
            k = wk["issued"]
            if k >= total_blocks:
                return
            wk["issued"] += 1
            s = k % NW
            dst = R(WS[s].ap.rearrange("p a b -> p (a b)"), WS[s]._keys(0, 16, 0, 128))
            dma("pool", dst, R(wst[k % NB]), "w%d" % s)

        def w_next(desc):
            k = wk["k"]
            assert ORDER[k % NB] == desc, (ORDER[k % NB], desc)
            wk["k"] += 1
            return WS[k % NW]

        def wv16(ws, kc):
            return ws.v(kc)

        def wv4(ws, kc, c0, c1):
            ap = ws.ap.rearrange("p a b -> p (a b)")[:, kc * 512 + c0: kc * 512 + c1]
            return R(ap, ws._keys(0, 16, 0, 128))

        psr = {"i": 0, "lo": 0, "od": 0}

        def ps_next(n=1):
            b = psr["i"]
            if b + n > 8:
                b = 0
            psr["i"] = (b + n) % 8
            return list(range(b, b + n))

        def ps_lo():
            b = psr["lo"]
            psr["lo"] = (b + 1) % 4
            return b

        def ps_od():
            b = 4 + 2 * psr["od"]
            psr["od"] ^= 1
            return b, b + 1

        dma("sp", VEC.v(0), R(vecs_d), "c0")
        dma("sp", CST.v(0), R(consts_d), "c1")
        for i in range(NW):
            w_issue()
        ident_f = CST.v(0, 0, 128)
        act(IDB.v(0), CST.v(0, 0, 128), AF.Copy)
        act(MSK.v(0), CST.v(0, 128, 256), AF.Copy)
        act(FLD.v(0), CST.v(0, 256, 384), AF.Copy)
        vmemset(ONE.v(0), 1.0)
        vmemset(CKV.v3(0, 4), 0.0)
        vmemset(KR.v(0), 0.0)
        vmemset(CB.v3(0, NCH, 0, 32), 0.0)
        identb, maskb, foldb, ones = IDB.v(0), MSK.v(0), FLD.v(0), ONE.v(0)
        eps_c = vcol("eps")

        xs_use = [0]

        def sumsq_rstd(src_fn, nch, slabs, dim, tmp_i=0):
            for (c0, n) in slabs:
                (b,) = ps_next(1)
                for ch in range(nch):
                    sq = TMPS[(tmp_i + ch) % 2]
                    sqv = R(sq.ap[:, 0, 0:n].bitcast(BF16)[:, 0:n], sq._keys(0, 1, 0, n))
                    act(sqv, src_fn(ch, c0, n), AF.Square)
                    mm(psb(b, 0, n), ones, sqv, ch == 0, ch == nch - 1, signal=True)
                act(RSTD.v(0, c0, c0 + n), psb(b, 0, n), AF.Sqrt, bias=eps_c, scale=1.0 / dim)
                recip(RSTD.v(0, c0, c0 + n), RSTD.v(0, c0, c0 + n))

        def gemm16(ws, rhs_fn, slabs, banks):
            for kc in range(16):
                for si, (c0, n) in enumerate(slabs):
                    mm(psb(banks[si], 0, n), wv16(ws, kc), rhs_fn(kc, c0, n), kc == 0, kc == 15)

        for ti, (t0, T) in enumerate(tiles):
            slabs = [(c, min(SL, T - c)) for c in range(0, T, SL)]
            nblk = (T + 127) // 128
            NS = len(slabs)

            dma("sp", GMX.v(0), R(gmix_d.partition_broadcast(128)), "c2")
            dma("sp", CSL.v(0, 0, T), R(cs_d[:, t0:t0 + T]), "c3")
            for j in range(nblk):
                rows = min(128, T - j * 128)
                xs = XS[xs_use[0] % 2]
                xsem = "x%d" % (xs_use[0] % 2)
                xs_use[0] += 1
                if rows < 128:
                    vmemset(xs.v(0), 0.0)
                dma("sp", xs.v(0, 0, D, 0, rows), R(h0[t0 + j * 128: t0 + j * 128 + rows, :]), xsem)
                ss = SSB.v(0, 0, 1)
                vmemset(ss, 0.0)
                act(JUNK.v(0), xs.v(0), AF.Square, accum=ss)
                act(SSB.v(0, 1, 2), ss, AF.Sqrt, bias=eps_c, scale=1.0 / D)
                recip(SSB.v(0, 2, 3), SSB.v(0, 1, 2))
                stt(UTM.v(0), xs.v(0), SSB.v(0, 2, 3), GMX.v(0), ALU.mult, ALU.mult)
                for half in range(2):
                    (b,) = ps_next(1)
                    pb = psb_bf(b)
                    for c in range(8):
                        ch = half * 8 + c
                        tr(R(pb.ap[:, c * 128:(c + 1) * 128], pb.keys), UTM.v(0, ch * 128, (ch + 1) * 128), identb)
                    src = R(pb.ap.rearrange("p (c w) -> p c w", c=8), pb.keys)
                    copy_any(B1.v3(half * 8, half * 8 + 8, j * 128, (j + 1) * 128), src)

            def u_rhs(kc, c0, n):
                return B1.v(kc, c0, c0 + n)
            for name, dst, gname, col0 in (("cq", CQ, "g_q", 0), ("ckv", CKV, "g_kv", t0)):
                for j in range(4):
                    ws = w_next(("in", name, j))
                    banks = ps_next(NS)
                    gemm16(ws, u_rhs, slabs, banks)
                    w_issue()
                    for si, (c0, n) in enumerate(slabs):
                        copy_any(dst.v(j, col0 + c0, col0 + c0 + n), psb(banks[si], 0, n))
                sumsq_rstd(lambda ch, c0, n, dst=dst, col0=col0: dst.v(ch, col0 + c0, col0 + c0 + n), 4, slabs, 512.0)
                for j in range(4):
                    for (c0, n) in slabs:
                        d_ = dst.v(j, col0 + c0, col0 + c0 + n)
                        stt(d_, d_, vcol(gname, j), RSTD.v(0, c0, c0 + n), ALU.mult, ALU.mult)
            ws = w_next(("in", "kr", 0))
            banks = ps_next(NS)
            gemm16(ws, u_rhs, slabs, banks)
            w_issue()
            for si, (c0, n) in enumerate(slabs):
                tmp = TMPS[2 + si % 2]
                tv = R(tmp.ap[:, 0, 0:n].bitcast(BF16)[:, 0:n], tmp._keys(0, 1, 0, n))
                tt(tv, psb(banks[si], 0, n), CSL.v(0, c0, c0 + n), ALU.mult)
                (b,) = ps_next(1)
                mm(psb(b, 0, n), foldb, tv, True, True)
                copy_any(KR.v(0, t0 + c0, t0 + c0 + n), psb(b, 0, n))

            for j in range(16):
                wa = w_next(("in", "za", j))
                ba = ps_next(NS)
                gemm16(wa, u_rhs, slabs, ba)
                w_issue()
                wb = w_next(("in", "zb", j))
                bb = ps_next(NS)
                gemm16(wb, u_rhs, slabs, bb)
                w_issue()
                for si, (c0, n) in enumerate(slabs):
                    sg = TMPS[si % 2].v(0, 0, n)
                    act(sg, psb(bb[si], 0, n), AF.Sigmoid, bias=vcol("b_zb", j))
                    stt(CB.v(j, 32 + c0, 32 + c0 + n), psb(ba[si], 0, n), vcol("b_za", j), sg, ALU.add, ALU.mult)
                acc = CACC[j % 2].v(0, 0, T)
                bgq.append(lambda acc=acc, ch=j: tsc(acc, CB.v(ch, 2, 2 + T), vcol("w_dw", ch), vcol("b_dw", ch), ALU.mult, ALU.add))
                for jj in range(1, 31):
                    dst = acc if jj < 30 else B3.v(j, 0, T)
                    bgq.append(lambda acc=acc, ch=j, jj=jj, dst=dst: stt(dst, CB.v(ch, 2 + jj, 2 + jj + T), vcol("w_dw", jj * 16 + ch), acc, ALU.mult, ALU.add))
                if j >= 1:
                    drain(CONV_DRAIN_B2)
            bgq.append(lambda: vcopy(CB.v3(0, NCH, 0, 32), CB.v3(0, NCH, T, T + 32)))
            att_slots = [16 * NS * 2]

            kend = t0 + T
            nkt = (kend + 127) // 128
            for g in range(4):
                wuv = w_next(("uv", g))
                for kt in range(nkt):
                    b = ps_lo()
                    for kc in range(4):
                        mm(psb(b), CKV.v(kc, kt * 128, kt * 128 + 128), wv4(wuv, kc, 0, 512), kc == 0, kc == 3)
                    act(VG.v(kt), psb(b), AF.Copy)
                w_issue()
                wuk = w_next(("uk", g))
                wq = [w_next(("uq", 2 * g)), w_next(("uq", 2 * g + 1))]
                for hh in range(4):
                    h = g * 4 + hh
                    kcols = nkt * 128
                    for k0 in range(0, kcols, 512):
                        kn = min(512, kcols - k0)
                        b = ps_lo()
                        for kc in range(4):
                            mm(psb(b, 0, kn), wv4(wuk, kc, hh * 128, hh * 128 + 128), CKV.v(kc, k0, k0 + kn), kc == 0, kc == 3)
                        act(KH.v(0, k0, k0 + kn), psb(b, 0, kn), AF.Copy)
                    wqs = wq[hh // 2]
                    qo = (hh % 2) * 256
                    for (c0, n) in slabs:
                        b = ps_lo()
                        for kc in range(4):
                            mm(psb(b, 0, n), wv4(wqs, kc, qo, qo + 128), CQ.v(kc, c0, c0 + n), kc == 0, kc == 3)
                        act(QN.v(0, c0, c0 + n), psb(b, 0, n), AF.Copy)
                        b2 = ps_lo()
                        for kc in range(4):
                            mm(psb(b2, 0, n), wv4(wqs, kc, qo + 128, qo + 256), CQ.v(kc, c0, c0 + n), kc == 0, kc == 3)
                        tt(QR.v(0, c0, c0 + n), psb(b2, 0, n), CSL.v(0, c0, c0 + n), ALU.mult)
                        drain(-(-len(bgq) // max(1, att_slots[0])))
                        att_slots[0] -= 1
                    for (c0, n) in slabs:
                        q0 = t0 + c0
                        nk = (q0 + n - 1) // 128 + 1
                        bo, bd = ps_od()
                        blocks = []
                        for kt in range(nk):
                            qs = max(q0, 128 * kt)
                            blocks.append((kt, qs - q0, n - (qs - q0), 128 * kt >= q0))

                        def qk(i):
                            kt, cs_, N, diag = blocks[i]
                            b = ps_lo()
                            mm(psb(b, 0, N), KH.v(0, kt * 128, kt * 128 + 128), QN.v(0, c0 + cs_, c0 + cs_ + N), True, False)
                            mm(psb(b, 0, N), KR.v(0, kt * 128, kt * 128 + 128), QR.v(0, c0 + cs_, c0 + cs_ + N), False, not diag)
                            if diag:
                                w_ = min(128, N)
                                mm(psb(b, 0, w_), identb, MSK.v(0, 0, w_), False, True)
                            return b
                        bs = qk(0)
                        for i, (kt, cs_, N, diag) in enumerate(blocks):
                            bnext = qk(i + 1) if i + 1 < len(blocks) else None
                            pt = PT[i % 3].v(0, 0, N)
                            act(pt, psb(bs, 0, N), AF.Exp, scale=SCALE)
                            mm(psb(bo, cs_, cs_ + N), VG.v(kt, hh * 128, hh * 128 + 128), pt, i == 0, i == len(blocks) - 1, signal=True)
                            mm(psb(bd, cs_, cs_ + N), ones, pt, i == 0, i == len(blocks) - 1, signal=True)
                            bs = bnext
                        recip(RDEN.v(0, 0, n), psb(bd, 0, n))
                        tt(B4.v(h, c0, c0 + n), psb(bo, 0, n), RDEN.v(0, 0, n), ALU.mult)
                        drain(-(-len(bgq) // max(1, att_slots[0])))
                        att_slots[0] -= 1
                w_issue()
                w_issue()
                w_issue()
            drain()

            for (c0, n) in slabs:
                (bm,) = ps_next(1)
                (bq,) = ps_next(1)
                for ch in range(16):
                    sq = TMPS[ch % 2]
                    sqv = R(sq.ap[:, 0, 0:n].bitcast(BF16)[:, 0:n], sq._keys(0, 1, 0, n))
                    act(sqv, B3.v(ch, c0, c0 + n), AF.Square)
                    mm(psb(bm, 0, n), ones, B3.v(ch, c0, c0 + n), ch == 0, ch == 15, signal=True)
                    mm(psb(bq, 0, n), ones, sqv, ch == 0, ch == 15, signal=True)
                mean = MEAN.v(0, c0, c0 + n)
                var = VAR.v(0, c0, c0 + n)
                rstd = RSTD.v(0, c0, c0 + n)
                tsc(mean, psb(bm, 0, n), 1.0 / D, None, ALU.mult)
                tt(var, mean, mean, ALU.mult)
                stt(var, psb(bq, 0, n), 1.0 / D, var, ALU.mult, ALU.subtract)
                act(rstd, var, AF.Sqrt, bias=eps_c, scale=1.0)
                recip(rstd, rstd)
                for ch in range(16):
                    t1 = TMPS[2 + ch % 2].v(0, 0, n)
                    tt(t1, B3.v(ch, c0, c0 + n), mean, ALU.subtract)
                    stt(t1, t1, vcol("g_ln", ch), rstd, ALU.mult, ALU.mult)
                    act(CB.v(ch, 32 + c0, 32 + c0 + n), t1, AF.Silu, bias=vcol("b_ln", ch))

            for (wname, gname, bname, src_fn, first) in (
                    ("ao", "ga", "b_ga", lambda kc, c0, n: B4.v(kc, c0, c0 + n), True),
                    ("co", "gc", "b_gc", lambda kc, c0, n: CB.v(kc, 32 + c0, 32 + c0 + n), False)):
                for j in range(16):
                    wa = w_next((wname, j))
                    ba = ps_next(NS)
                    gemm16(wa, src_fn, slabs, ba)
                    w_issue()
                    wg = w_next(("in", gname, j))
                    bg = ps_next(NS)
                    gemm16(wg, u_rhs, slabs, bg)
                    w_issue()
                    for si, (c0, n) in enumerate(slabs):
                        sg = TMPS[si % 2].v(0, 0, n)
                        act(sg, psb(bg[si], 0, n), AF.Sigmoid, bias=vcol(bname, j))
                        if first:
                            tt(B3.v(j, c0, c0 + n), psb(ba[si], 0, n), sg, ALU.mult)
                        else:
                            t1 = TMPS[2 + si % 2].v(0, 0, n)
                            stt(t1, psb(ba[si], 0, n), vcol("b_co", j), sg, ALU.add, ALU.mult)
                            tt(B3.v(j, c0, c0 + n), B3.v(j, c0, c0 + n), t1, ALU.add)

            for j in range(nblk):
                rows = min(128, T - j * 128)
                xs = XS[xs_use[0] % 2]
                xsem = "x%d" % (xs_use[0] % 2)
                xs_use[0] += 1
                if rows < 128:
                    vmemset(xs.v(0), 0.0)
                dma("sp", xs.v(0, 0, D, 0, rows), R(h0[t0 + j * 128: t0 + j * 128 + rows, :]), xsem)
                for q4 in range(4):
                    (b,) = ps_next(1)
                    for c in range(4):
                        ch = q4 * 4 + c
                        tr(psb(b, c * 128, (c + 1) * 128), xs.v(0, ch * 128, (ch + 1) * 128), ident_f)
                    src = R(ps_t[:, b, :].rearrange("p (c w) -> p c w", c=4), [("ps", b)])
                    copy_any(H.v3(q4 * 4, q4 * 4 + 4, j * 128, (j + 1) * 128), src)
            for j in range(16):
                ws = w_next(("wo", j))
                banks = ps_next(NS)
                gemm16(ws, lambda kc, c0, n: B3.v(kc, c0, c0 + n), slabs, banks)
                w_issue()
                for si, (c0, n) in enumerate(slabs):
                    tt(H.v(j, c0, c0 + n), psb(banks[si], 0, n), H.v(j, c0, c0 + n), ALU.add)

            sumsq_rstd(lambda ch, c0, n: H.v(ch, c0, c0 + n), 16, slabs, float(D))
            for ch in range(16):
                for (c0, n) in slabs:
                    stt(B3.v(ch, c0, c0 + n), H.v(ch, c0, c0 + n), vcol("g_ffn", ch), RSTD.v(0, c0, c0 + n), ALU.mult, ALU.mult)

            def hn_rhs(kc, c0, n):
                return B3.v(kc, c0, c0 + n)
            for g in range(NFG):
                ag = ACTG[g % 2]
                for f in range(4):
                    wg = w_next(("fg", g * 4 + f))
                    bg = ps_next(NS)
                    gemm16(wg, hn_rhs, slabs, bg)
                    w_issue()
                    wu = w_next(("fu", g * 4 + f))
                    bu = ps_next(NS)
                    gemm16(wu, hn_rhs, slabs, bu)
                    w_issue()
                    for si, (c0, n) in enumerate(slabs):
                        sg = TMPS[si % 2].v(0, 0, n)
                        act(sg, psb(bg[si], 0, n), AF.Silu)
                        tt(ag.v(f, c0, c0 + n), psb(bu[si], 0, n), sg, ALU.mult)
                for cb in range(4):
                    wd = w_next(("fd", g, cb))
                    for o4 in range(4):
                        oc = cb * 4 + o4
                        banks = ps_next(NS)
                        for kc in range(4):
                            for si, (c0, n) in enumerate(slabs):
                                mm(psb(banks[si], 0, n), wv4(wd, kc, o4 * 128, o4 * 128 + 128), ag.v(kc, c0, c0 + n), kc == 0, kc == 3)
                        for si, (c0, n) in enumerate(slabs):
                            tt(H.v(oc, c0, c0 + n), psb(banks[si], 0, n), H.v(oc, c0, c0 + n), ALU.add)
                    w_issue()

            sumsq_rstd(lambda ch, c0, n: H.v(ch, c0, c0 + n), 16, slabs, float(D))
            for ch in range(16):
                for (c0, n) in slabs:
                    stt(H.v(ch, c0, c0 + n), H.v(ch, c0, c0 + n), vcol("g_fin", ch), RSTD.v(0, c0, c0 + n), ALU.mult, ALU.mult)
            for j in range(nblk):
                rows = min(128, T - j * 128)
                xs = XS[xs_use[0] % 2]
                xsem = "x%d" % (xs_use[0] % 2)
                xs_use[0] += 1
                for q4 in range(4):
                    (b,) = ps_next(1)
                    for c in range(4):
                        ch = q4 * 4 + c
                        tr(psb(b, c * 128, (c + 1) * 128), H.v(ch, j * 128, (j + 1) * 128), ident_f)
                    copy_any(xs.v(0, q4 * 512, (q4 + 1) * 512), psb(b))
                tok0 = t0 + j * 128
                p0 = max(0, NMETA - tok0)
                if rows > p0:
                    dma("sp", R(out_d[tok0 + p0 - NMETA: tok0 + rows - NMETA, :]), xs.v(0, 0, D, p0, rows), xsem)

        assert wk["k"] == total_blocks and wk["issued"] == total_blocks, (wk, total_blocks)

        dma_sems = sorted(S.dma_count.keys())
        sem_names = list(S.ops.keys()) + dma_sems
        semh = {n: es.enter_context(nc.semaphore("s_" + n)) for n in sem_names}
        for k, v in S.maxwait.items():
            lim = S.count[k] if k in S.count else S.dma_count[k]
            assert v <= lim, ("wait beyond final count", k, v, lim)
        block = es.enter_context(nc.Block())

        def run(e, name):
            for fn, waits, signal, dsem in S.ops[name]:
                for (k, v) in waits:
                    e.wait_ge(semh[k], v)
                ins = fn(e)
                if signal:
                    ins.then_inc(semh[name], 1)
                if dsem is not None:
                    ins.then_inc(semh[dsem], 16)
            if name == "sp":
                for k in dma_sems:
                    if k.startswith("x") or k.startswith("c"):
                        e.wait_ge(semh[k], S.dma_count[k])

        @block.tensor
        def _(e):
            run(e, "pe")

        @block.scalar
        def _(e):
            run(e, "act")

        @block.vector
        def _(e):
            run(e, "dve")

        @block.gpsimd
        def _(e):
            run(e, "pool")

        @block.sync
        def _(e):
            run(e, "sp")
    stats = {n: len(v) for n, v in S.ops.items()}
    return nc, stats


_CACHE = {}


def kernel(x, meta_tokens, g_mix, w_in, b_glu, b_gate, w_dw, b_dw, g_conv_ln, b_conv_ln, w_conv_out, b_conv_out,
           g_q_lora, w_uq, g_kv_lora, w_uk, w_uv, w_attn_out, w_out, g_ffn, w_ffn_gate, w_ffn_up, w_ffn_down, g_final):
    f = lambda a: np.asarray(a, np.float32)
    x = f(x)
    meta = f(meta_tokens)
    wst = pack_weights(f(w_in)[0], f(w_conv_out)[0], f(w_uq)[0], f(w_uk)[0], f(w_uv)[0], f(w_attn_out)[0], f(w_out)[0],
                       f(w_ffn_gate)[0], f(w_ffn_up)[0], f(w_ffn_down)[0])
    vecs = pack_vecs(f(b_glu)[0], f(b_gate)[0], f(b_dw)[0], f(g_conv_ln)[0], f(b_conv_ln)[0], f(b_conv_out)[0],
                     f(g_q_lora)[0], f(g_kv_lora)[0], f(g_ffn)[0], f(g_final), f(w_dw)[0])
    consts, cs = const_tables()
    gmix = np.ascontiguousarray(f(g_mix)[0])
    if "nc" not in _CACHE:
        _CACHE["nc"] = build_program()[0]
    nc = _CACHE["nc"]
    nb = x.shape[0]
    in_maps = []
    for b in range(nb):
        h0 = np.ascontiguousarray(np.concatenate([meta, x[b]], axis=0))
        in_maps.append({"h0": h0, "wst": wst, "vecs": vecs, "consts": consts, "cs": cs, "gmix": gmix})
    res = run_bass_kernel_spmd(nc, in_maps, core_ids=list(range(nb)))
    return np.stack([np.asarray(r["out"], np.float32) for r in res.results], axis=0)
```

```python
import numpy as np
import concourse.bass as bass
import concourse.mybir as mybir
from concourse.bass_utils import run_bass_kernel_spmd

F32, BF16 = mybir.dt.float32, mybir.dt.bfloat16
AF = mybir.ActivationFunctionType
ALU = mybir.AluOpType

D = 2048
NCH = 16
NMETA = 16
SEQ = 2048
L = NMETA + SEQ
LP = 2176
TILES = [(0, 768), (768, 768), (1536, 528)]
SL = 384
TB = 768
DFF = 5632
NFG = 11
NH = 16
SCALE = float(192 ** -0.5)
EPS = 1e-6
NW = 4
CONV_DRAIN_B2 = 7
GR = 256


def stream_order():
    o = []
    for j in range(4):
        o.append(("in", "cq", j))
    for j in range(4):
        o.append(("in", "ckv", j))
    o.append(("in", "kr", 0))
    for j in range(16):
        o.append(("in", "za", j))
        o.append(("in", "zb", j))
    for g in range(4):
        o.append(("uv", g))
        o.append(("uq", 2 * g))
        o.append(("uk", g))
        o.append(("uq", 2 * g + 1))
    for j in range(16):
        o.append(("ao", j))
        o.append(("in", "ga", j))
    for j in range(16):
        o.append(("co", j))
        o.append(("in", "gc", j))
    for j in range(16):
        o.append(("wo", j))
    for g in range(NFG):
        for f in range(4):
            o.append(("fg", g * 4 + f))
            o.append(("fu", g * 4 + f))
        for cb in range(4):
            o.append(("fd", g, cb))
    return o


ORDER = stream_order()
NB = len(ORDER)

VOFF = {}
_c = 0
for _n, _w in [("b_za", 16), ("b_zb", 16), ("b_gc", 16), ("b_ga", 16), ("b_dw", 16), ("g_ln", 16), ("b_ln", 16),
               ("b_co", 16), ("g_q", 4), ("g_kv", 4), ("g_ffn", 16), ("g_fin", 16), ("eps", 1), ("w_dw", 31 * 16)]:
    VOFF[_n] = _c
    _c += _w
NV = _c


def _pack_k2048(W, cols):
    return np.ascontiguousarray(W[:, cols].reshape(16, 128, 128).transpose(1, 0, 2).reshape(128, 2048))


def _pack_k512(W, cols):
    return np.ascontiguousarray(W[:, cols].reshape(4, 128, 512).transpose(1, 0, 2).reshape(128, 2048))


def pack_weights(w_in, w_conv_out, w_uq, w_uk, w_uv, w_attn_out, w_out, w_ffn_gate, w_ffn_up, w_ffn_down):
    wst = np.empty((NB, 128, 2048), np.float32)
    ar = np.arange
    base = {"za": 0, "zb": 2048, "cq": 4096, "ckv": 4608, "gc": 5184, "ga": 7232}
    for i, d in enumerate(ORDER):
        t = d[0]
        if t == "in":
            if d[1] == "kr":
                r = 5120 + ar(64)
                sw = 5120 + np.concatenate([ar(32, 64), ar(0, 32)])
                cols = np.concatenate([r, sw])
            else:
                cols = base[d[1]] + d[2] * 128 + ar(128)
            wst[i] = _pack_k2048(w_in, cols)
        elif t == "co":
            wst[i] = _pack_k2048(w_conv_out, d[1] * 128 + ar(128))
        elif t == "ao":
            wst[i] = _pack_k2048(w_attn_out, d[1] * 128 + ar(128))
        elif t == "wo":
            wst[i] = _pack_k2048(w_out, d[1] * 128 + ar(128))
        elif t == "fg":
            wst[i] = _pack_k2048(w_ffn_gate, d[1] * 128 + ar(128))
        elif t == "fu":
            wst[i] = _pack_k2048(w_ffn_up, d[1] * 128 + ar(128))
        elif t == "fd":
            g, cb = d[1], d[2]
            blk = w_ffn_down[g * 512:(g + 1) * 512, cb * 512:(cb + 1) * 512]
            wst[i] = blk.reshape(4, 128, 512).transpose(1, 0, 2).reshape(128, 2048)
        elif t == "uv":
            wst[i] = _pack_k512(w_uv, d[1] * 512 + ar(512))
        elif t == "uk":
            wst[i] = _pack_k512(w_uk, d[1] * 512 + ar(512))
        elif t == "uq":
            cols = []
            for h in (2 * d[1], 2 * d[1] + 1):
                b = h * 192
                cols += [b + ar(128), b + 128 + ar(64), b + 128 + ar(32, 64), b + 128 + ar(0, 32)]
            wst[i] = _pack_k512(w_uq, np.concatenate(cols))
        else:
            raise ValueError(t)
    return wst


def _col(v):
    return np.ascontiguousarray(np.asarray(v, np.float32).reshape(-1, 128).T)


def pack_vecs(b_glu, b_gate, b_dw, g_conv_ln, b_conv_ln, b_conv_out, g_q_lora, g_kv_lora, g_ffn, g_final, w_dw):
    v = np.zeros((128, NV), np.float32)

    def put(name, arr):
        c = _col(arr)
        v[:, VOFF[name]:VOFF[name] + c.shape[1]] = c
    put("b_za", b_glu[:2048])
    put("b_zb", b_glu[2048:])
    put("b_gc", b_gate[:2048])
    put("b_ga", b_gate[2048:])
    put("b_dw", b_dw)
    put("g_ln", g_conv_ln)
    put("b_ln", b_conv_ln)
    put("b_co", b_conv_out)
    put("g_q", g_q_lora)
    put("g_kv", g_kv_lora)
    put("g_ffn", g_ffn)
    put("g_fin", g_final)
    v[:, VOFF["eps"]] = EPS
    for j in range(31):
        c = _col(w_dw[j])
        v[:, VOFF["w_dw"] + j * 16: VOFF["w_dw"] + (j + 1) * 16] = c
    return v


def const_tables():
    p = np.arange(128)
    ident = np.eye(128, dtype=np.float32)
    maskneg = np.where(p[:, None] <= p[None, :], 0.0, -30000.0).astype(np.float32)
    fold = (p[:, None] % 64 == p[None, :] % 64).astype(np.float32)
    consts = np.concatenate([ident, maskneg, fold], axis=1)
    pos = np.arange(LP, dtype=np.float32)
    inv_freq = (np.float32(10000.0) ** (-np.arange(0, 64, 2, dtype=np.float32) / np.float32(64))).astype(np.float32)
    ang = (pos[None, :] * inv_freq[:, None]).astype(np.float32)
    cos, sin = np.cos(ang).astype(np.float32), np.sin(ang).astype(np.float32)
    cs = np.concatenate([cos, cos, -sin, sin], axis=0).astype(np.float32)
    return np.ascontiguousarray(consts), np.ascontiguousarray(cs)


class R:
    __slots__ = ("ap", "keys")

    def __init__(self, ap, keys=()):
        self.ap = ap
        self.keys = frozenset(keys)


class Sched:
    def __init__(self):
        self.ops = {n: [] for n in ("pe", "act", "dve", "pool", "sp")}
        self.count = {n: 0 for n in self.ops}
        self.waited = {n: {} for n in self.ops}
        self.last_w = {}
        self.readers = {}
        self.dma_count = {}
        self.maxwait = {}
        self.stage = ''
        self.tags = {n: [] for n in self.ops}

    def op(self, eng, fn, reads=(), writes=(), signal=True, dma_sem=None):
        deps = {}

        def need(ev):
            if ev is not None and deps.get(ev[0], 0) < ev[1]:
                deps[ev[0]] = ev[1]
        rk = set()
        for r in reads:
            rk |= r.keys
        wk = set()
        for w in writes:
            wk |= w.keys
        for k in rk:
            need(self.last_w.get(k))
        for k in wk:
            need(self.last_w.get(k))
            rd = self.readers.get(k)
            if rd:
                for s, v in rd.items():
                    need((s, v))
        if dma_sem is not None:
            self.dma_count[dma_sem] = self.dma_count.get(dma_sem, 0) + 16
            ev = (dma_sem, self.dma_count[dma_sem])
            signal = False
        elif signal:
            self.count[eng] += 1
            ev = (eng, self.count[eng])
        else:
            ev = (eng, self.count[eng] + 1)
        waits = []
        wd = self.waited[eng]
        for k, v in deps.items():
            if k == eng:
                continue
            if wd.get(k, 0) >= v:
                continue
            wd[k] = v
            waits.append((k, v))
            if self.maxwait.get(k, 0) < v:
                self.maxwait[k] = v
        self.ops[eng].append((fn, waits, signal, dma_sem))
        self.tags[eng].append(self.stage)
        for k in rk:
            d = self.readers.setdefault(k, {})
            if d.get(ev[0], 0) < ev[1]:
                d[ev[0]] = ev[1]
        for k in wk:
            self.last_w[k] = ev
            self.readers[k] = {}
        return ev


class Buf:
    def __init__(self, arena, off, dtype, n0, n1, tag):
        self.off, self.dtype, self.n0, self.n1, self.tag = off, dtype, n0, n1, tag
        self.es = 4 if dtype == F32 else 2
        nb = n0 * n1 * self.es
        a = arena[:, off // 2: (off + nb) // 2]
        if dtype == F32:
            a = a.bitcast(F32)
        self.ap = a.rearrange("p (a b) -> p a b", a=n0)

    def _keys(self, i0, i1, c0, c1):
        ks = set()
        for i in range(i0, i1):
            lo = self.off + (i * self.n1 + c0) * self.es
            hi = self.off + (i * self.n1 + c1) * self.es
            ks.update(("sb", g) for g in range(lo // GR, (hi - 1) // GR + 1))
        return ks

    def v(self, i, c0=0, c1=None, p0=0, p1=128):
        c1 = self.n1 if c1 is None else c1
        return R(self.ap[p0:p1, i, c0:c1], self._keys(i, i + 1, c0, c1))

    def v3(self, i0, i1, c0=0, c1=None):
        c1 = self.n1 if c1 is None else c1
        return R(self.ap[:, i0:i1, c0:c1], self._keys(i0, i1, c0, c1))


def build_program(ntiles=3, dbg=None):
    nc = bass.Bass("TRN2", target_bir_lowering=False)
    h0 = nc.dram_tensor("h0", [L, D], F32, kind="ExternalInput").ap()
    wst = nc.dram_tensor("wst", [NB, 128, 2048], F32, kind="ExternalInput").ap()
    vecs_d = nc.dram_tensor("vecs", [128, NV], F32, kind="ExternalInput").ap()
    consts_d = nc.dram_tensor("consts", [128, 384], F32, kind="ExternalInput").ap()
    cs_d = nc.dram_tensor("cs", [128, LP], F32, kind="ExternalInput").ap()
    gmix_d = nc.dram_tensor("gmix", [D], F32, kind="ExternalInput").ap()
    out_d = nc.dram_tensor("out", [SEQ, D], F32, kind="ExternalOutput").ap()
    dbg_d = {}
    if dbg:
        for name, shape in dbg.items():
            dbg_d[name] = nc.dram_tensor("dbg_" + name, list(shape), F32, kind="ExternalOutput").ap()

    tiles = TILES[:ntiles]
    S = Sched()
    ARENA_BYTES = 207 * 1024 + 512
    import contextlib
    with contextlib.ExitStack() as es:
        arena = es.enter_context(nc.sbuf_tensor("arena", [128, ARENA_BYTES // 2], BF16))
        ps_t = es.enter_context(nc.psum_tensor("ps", [128, 8, 512], F32))
        cur = [0]

        def alloc(nbytes):
            o = cur[0]
            cur[0] = (o + nbytes + GR - 1) // GR * GR
            assert cur[0] <= ARENA_BYTES, ("SBUF overflow", cur[0])
            return o

        def mk(dtype, n0, n1, tag, off=None):
            es_ = 4 if dtype == F32 else 2
            if off is None:
                off = alloc(n0 * n1 * es_)
            return Buf(arena, off, dtype, n0, n1, tag)

        o_bb = alloc(2 * NCH * TB * 2)
        B1 = mk(BF16, NCH, TB, "B1", o_bb)
        B4 = mk(BF16, NCH, TB, "B4", o_bb + NCH * TB * 2)
        H = mk(F32, NCH, TB, "H", o_bb)
        CBW = 32 + TB
        CB = mk(BF16, NCH, CBW, "CB")
        B3 = mk(BF16, NCH, TB, "B3")
        CQ = mk(BF16, 4, TB, "CQ")
        CKV = mk(BF16, 4, LP, "CKV")
        KR = mk(BF16, 1, LP, "KR")
        o_sa = alloc(36352)
        XS = [mk(F32, 1, D, "XS%d" % i, o_sa + i * 8192) for i in range(2)]
        UTM = mk(BF16, 1, D, "UTM", o_sa + 16384)
        JUNK = mk(BF16, 1, D, "JUNK", o_sa + 20480)
        GMX = mk(F32, 1, D, "GMX", o_sa + 24576)
        ACTG = [mk(BF16, 4, TB, "ACTG%d" % i, o_sa + 16384 + i * 4 * TB * 2) for i in range(2)]
        VG = mk(BF16, 17, 512, "VG", o_sa)
        KH = [mk(BF16, 1, LP, "KH%d" % i, o_sa + 17408 + i * 4352) for i in range(2)]
        QN = [mk(BF16, 1, TB, "QN%d" % i, o_sa + 26112 + i * 1536) for i in range(2)]
        QR = [mk(BF16, 1, TB, "QR%d" % i, o_sa + 29184 + i * 1536) for i in range(2)]
        PT = [mk(BF16, 1, SL, "PT%d" % i, o_sa + 32256 + i * 768) for i in range(3)]
        RDEN = mk(F32, 1, SL, "RDEN", o_sa + 34560)
        assert 34560 + 1536 <= 36352
        CACC = [mk(F32, 1, TB, "CACC%d" % i) for i in range(2)]
        TMPS = [mk(F32, 1, SL, "TMP%d" % i) for i in range(4)]
        MEAN = mk(F32, 1, TB, "MEAN")
        RSTD = mk(F32, 1, TB, "RSTD")
        VAR = mk(F32, 1, TB, "VAR")
        QRAW = [mk(BF16, 1, TB, "QRAW0", MEAN.off), mk(BF16, 1, TB, "QRAW1", VAR.off)]
        WS = [mk(BF16, 16, 128, "WS%d" % i) for i in range(NW)]
        VEC = mk(F32, 1, NV, "VEC")
        CSL = mk(F32, 1, TB, "CSL")
        CST = mk(F32, 1, 384, "CST")
        IDB = mk(BF16, 1, 128, "IDB")
        MSK = mk(BF16, 1, 128, "MSK")
        FLD = mk(BF16, 1, 128, "FLD")
        ONE = mk(BF16, 1, 128, "ONE")
        SSB = mk(F32, 1, 8, "SSB")

        def psb(b, c0=0, c1=512):
            return R(ps_t[:, b, c0:c1], [("ps", b)])

        def psb_bf(b):
            return R(ps_t[:, b, :].bitcast(BF16), [("ps", b)])

        def vcol(name, j=0):
            c = VOFF[name] + j
            return R(VEC.ap[:, 0, c:c + 1], VEC._keys(0, 1, c, c + 1))

        def mm(out, lhsT, rhs, start, stop, signal=None):
            S.op("pe", lambda e, o=out.ap, l=lhsT.ap, r=rhs.ap, a=start, b=stop: e.matmul(o, lhsT=l, rhs=r, start=a, stop=b),
                 reads=[lhsT, rhs], writes=[out], signal=(stop if signal is None else signal))

        def tr(out, in_, ident):
            S.op("pe", lambda e, o=out.ap, i=in_.ap, d=ident.ap: e.transpose(o, i, d), reads=[in_, ident], writes=[out])

        def act(out, in_, func, bias=None, scale=None, accum=None, extra_reads=()):
            kw = {}
            rd = [in_] + list(extra_reads)
            wr = [out]
            if bias is not None:
                kw["bias"] = bias.ap
                rd.append(bias)
            if scale is not None:
                if isinstance(scale, R):
                    kw["scale"] = scale.ap
                    rd.append(scale)
                else:
                    kw["scale"] = float(scale)
            if accum is not None:
                kw["accum_out"] = accum.ap
                wr.append(accum)
            S.op("act", lambda e, o=out.ap, i=in_.ap, f=func, kw=kw: e.activation(out=o, in_=i, func=f, **kw), reads=rd, writes=wr)

        def _sc(x, rd):
            if isinstance(x, R):
                rd.append(x)
                return x.ap
            return x

        def tsc(out, in0, s1, s2, op0, op1=None):
            rd = [in0]
            a1 = _sc(s1, rd)
            a2 = _sc(s2, rd) if s2 is not None else None
            if op1 is None:
                S.op("dve", lambda e, o=out.ap, i=in0.ap: e.tensor_scalar(out=o, in0=i, scalar1=a1, scalar2=None, op0=op0), reads=rd, writes=[out])
            else:
                S.op("dve", lambda e, o=out.ap, i=in0.ap: e.tensor_scalar(out=o, in0=i, scalar1=a1, scalar2=a2, op0=op0, op1=op1), reads=rd, writes=[out])

        def stt(out, in0, scalar, in1, op0, op1):
            rd = [in0, in1]
            a = _sc(scalar, rd)
            S.op("dve", lambda e, o=out.ap, i0=in0.ap, i1=in1.ap: e.scalar_tensor_tensor(out=o, in0=i0, scalar=a, in1=i1, op0=op0, op1=op1),
                 reads=rd, writes=[out])

        def tt(out, in0, in1, op):
            S.op("dve", lambda e, o=out.ap, i0=in0.ap, i1=in1.ap: e.tensor_tensor(out=o, in0=i0, in1=i1, op=op), reads=[in0, in1], writes=[out])

        def recip(out, in_):
            S.op("dve", lambda e, o=out.ap, i=in_.ap: e.reciprocal(out=o, in_=i), reads=[in_], writes=[out])

        def vcopy(out, in_):
            S.op("dve", lambda e, o=out.ap, i=in_.ap: e.tensor_copy(out=o, in_=i), reads=[in_], writes=[out])

        def vmemset(out, val):
            S.op("dve", lambda e, o=out.ap: e.memset(o, val), writes=[out])

        def dma(eng, out, in_, sem):
            S.op(eng, lambda e, o=out.ap, i=in_.ap: e.dma_start(out=o, in_=i), reads=[in_], writes=[out], dma_sem=sem)

        cp_flip = [0]

        def copy_any(out, in_):
            cp_flip[0] ^= 1
            if cp_flip[0]:
                act(out, in_, AF.Copy)
            else:
                vcopy(out, in_)

        import collections
        bgq = collections.deque()

        def drain(n=None):
            k = len(bgq) if n is None else min(n, len(bgq))
            for _ in range(k):
                bgq.popleft()()

        wk = {"issued": 0, "k": 0}
        total_blocks = NB * len(tiles)

        def w_issue():
            k = wk["issued"]
            if k >= total_blocks:
                return
            wk["issued"] += 1
            s = k % NW
            dst = R(WS[s].ap.rearrange("p a b -> p (a b)"), WS[s]._keys(0, 16, 0, 128))
            dma("pool", dst, R(wst[k % NB]), "w%d" % s)

        def w_next(desc):
            k = wk["k"]
            assert ORDER[k % NB] == desc, (ORDER[k % NB], desc)
            wk["k"] += 1
            return WS[k % NW]

        def wv16(ws, kc):
            return ws.v(kc)

        def wv4(ws, kc, c0, c1):
            ap = ws.ap.rearrange("p a b -> p (a b)")[:, kc * 512 + c0: kc * 512 + c1]
            return R(ap, ws._keys(0, 16, 0, 128))

        psr = {"i": 0, "lo": 0, "od": 0}

        def ps_next(n=1):
            b = psr["i"]
            if b + n > 8:
                b = 0
            psr["i"] = (b + n) % 8
            return list(range(b, b + n))

        def ps_lo():
            b = psr["lo"]
            psr["lo"] = (b + 1) % 4
            return b

        def ps_od():
            b = 4 + 2 * psr["od"]
            psr["od"] ^= 1
            return b, b + 1

        dma("sp", VEC.v(0), R(vecs_d), "c0")
        dma("sp", CST.v(0), R(consts_d), "c1")
        for i in range(NW):
            w_issue()
        ident_f = CST.v(0, 0, 128)
        act(IDB.v(0), CST.v(0, 0, 128), AF.Copy)
        act(MSK.v(0), CST.v(0, 128, 256), AF.Copy)
        act(FLD.v(0), CST.v(0, 256, 384), AF.Copy)
        vmemset(ONE.v(0), 1.0)
        vmemset(CKV.v3(0, 4), 0.0)
        vmemset(KR.v(0), 0.0)
        vmemset(CB.v3(0, NCH, 0, 32), 0.0)
        identb, maskb, foldb, ones = IDB.v(0), MSK.v(0), FLD.v(0), ONE.v(0)
        eps_c = vcol("eps")

        xs_use = [0]

        def sumsq_rstd(src_fn, nch, slabs, dim, tmp_i=0):
            for (c0, n) in slabs:
                (b,) = ps_next(1)
                for ch in range(nch):
                    sq = TMPS[(tmp_i + ch) % 2]
                    sqv = R(sq.ap[:, 0, 0:n].bitcast(BF16)[:, 0:n], sq._keys(0, 1, 0, n))
                    act(sqv, src_fn(ch, c0, n), AF.Square)
                    mm(psb(b, 0, n), ones, sqv, ch == 0, ch == nch - 1, signal=True)
                act(RSTD.v(0, c0, c0 + n), psb(b, 0, n), AF.Sqrt, bias=eps_c, scale=1.0 / dim)
                recip(RSTD.v(0, c0, c0 + n), RSTD.v(0, c0, c0 + n))

        def gemm16(ws, rhs_fn, slabs, banks):
            for kc in range(16):
                for si, (c0, n) in enumerate(slabs):
                    mm(psb(banks[si], 0, n), wv16(ws, kc), rhs_fn(kc, c0, n), kc == 0, kc == 15)

        for ti, (t0, T) in enumerate(tiles):
            slabs = [(c, min(SL, T - c)) for c in range(0, T, SL)]
            nblk = (T + 127) // 128
            NS = len(slabs)

            S.stage = "A"
            dma("sp", GMX.v(0), R(gmix_d.partition_broadcast(128)), "c2")
            dma("sp", CSL.v(0, 0, T), R(cs_d[:, t0:t0 + T]), "c3")
            for j in range(nblk):
                rows = min(128, T - j * 128)
                xs = XS[xs_use[0] % 2]
                xsem = "x%d" % (xs_use[0] % 2)
                xs_use[0] += 1
                if rows < 128:
                    vmemset(xs.v(0), 0.0)
                dma("sp", xs.v(0, 0, D, 0, rows), R(h0[t0 + j * 128: t0 + j * 128 + rows, :]), xsem)
                ss = SSB.v(0, 0, 1)
                vmemset(ss, 0.0)
                act(JUNK.v(0), xs.v(0), AF.Square, accum=ss)
                act(SSB.v(0, 1, 2), ss, AF.Sqrt, bias=eps_c, scale=1.0 / D)
                recip(SSB.v(0, 2, 3), SSB.v(0, 1, 2))
                stt(UTM.v(0), xs.v(0), SSB.v(0, 2, 3), GMX.v(0), ALU.mult, ALU.mult)
                for half in range(2):
                    (b,) = ps_next(1)
                    pb = psb_bf(b)
                    for c in range(8):
                        ch = half * 8 + c
                        tr(R(pb.ap[:, c * 128:(c + 1) * 128], pb.keys), UTM.v(0, ch * 128, (ch + 1) * 128), identb)
                    src = R(pb.ap.rearrange("p (c w) -> p c w", c=8), pb.keys)
                    copy_any(B1.v3(half * 8, half * 8 + 8, j * 128, (j + 1) * 128), src)

            S.stage = "B1"
            def u_rhs(kc, c0, n):
                return B1.v(kc, c0, c0 + n)
            for name, dst, gname, col0 in (("cq", CQ, "g_q", 0), ("ckv", CKV, "g_kv", t0)):
                for j in range(4):
                    ws = w_next(("in", name, j))
                    banks = ps_next(NS)
                    gemm16(ws, u_rhs, slabs, banks)
                    w_issue()
                    for si, (c0, n) in enumerate(slabs):
                        copy_any(dst.v(j, col0 + c0, col0 + c0 + n), psb(banks[si], 0, n))
                sumsq_rstd(lambda ch, c0, n, dst=dst, col0=col0: dst.v(ch, col0 + c0, col0 + c0 + n), 4, slabs, 512.0)
                for j in range(4):
                    for (c0, n) in slabs:
                        d_ = dst.v(j, col0 + c0, col0 + c0 + n)
                        stt(d_, d_, vcol(gname, j), RSTD.v(0, c0, c0 + n), ALU.mult, ALU.mult)
            ws = w_next(("in", "kr", 0))
            banks = ps_next(NS)
            gemm16(ws, u_rhs, slabs, banks)
            w_issue()
            for si, (c0, n) in enumerate(slabs):
                tmp = TMPS[2 + si % 2]
                tv = R(tmp.ap[:, 0, 0:n].bitcast(BF16)[:, 0:n], tmp._keys(0, 1, 0, n))
                tt(tv, psb(banks[si], 0, n), CSL.v(0, c0, c0 + n), ALU.mult)
                (b,) = ps_next(1)
                mm(psb(b, 0, n), foldb, tv, True, True)
                copy_any(KR.v(0, t0 + c0, t0 + c0 + n), psb(b, 0, n))

            S.stage = "B2"
            for j in range(16):
                wa = w_next(("in", "za", j))
                ba = ps_next(NS)
                gemm16(wa, u_rhs, slabs, ba)
                w_issue()
                drain(CONV_DRAIN_B2)
                wb = w_next(("in", "zb", j))
                bb = ps_next(NS)
                gemm16(wb, u_rhs, slabs, bb)
                w_issue()
                for si, (c0, n) in enumerate(slabs):
                    sg = TMPS[si % 2].v(0, 0, n)
                    act(sg, psb(bb[si], 0, n), AF.Sigmoid, bias=vcol("b_zb", j))
                    stt(CB.v(j, 32 + c0, 32 + c0 + n), psb(ba[si], 0, n), vcol("b_za", j), sg, ALU.add, ALU.mult)
                acc = CACC[j % 2].v(0, 0, T)
                bgq.append(lambda acc=acc, ch=j: tsc(acc, CB.v(ch, 2, 2 + T), vcol("w_dw", ch), vcol("b_dw", ch), ALU.mult, ALU.add))
                for jj in range(1, 31):
                    dst = acc if jj < 30 else B3.v(j, 0, T)
                    bgq.append(lambda acc=acc, ch=j, jj=jj, dst=dst: stt(dst, CB.v(ch, 2 + jj, 2 + jj + T), vcol("w_dw", jj * 16 + ch), acc, ALU.mult, ALU.add))
                drain(CONV_DRAIN_B2)
            bgq.append(lambda: vcopy(CB.v3(0, NCH, 0, 32), CB.v3(0, NCH, T, T + 32)))
            att_slots = [16 * NS * 2]

            S.stage = "D"
            kend = t0 + T
            nkt = (kend + 127) // 128
            ast = {}
            rd_i = [0]

            def prepare(h):
                g, hh = divmod(h, 4)
                if hh == 0:
                    wuv = w_next(("uv", g))
                    for kt in range(nkt):
                        b = ps_lo()
                        for kc in range(4):
                            mm(psb(b), CKV.v(kc, kt * 128, kt * 128 + 128), wv4(wuv, kc, 0, 512), kc == 0, kc == 3)
                        act(VG.v(kt), psb(b), AF.Copy)
                    w_issue()
                    wq0 = w_next(("uq", 2 * g))
                    ast["wuk"] = w_next(("uk", g))
                    ast["wq"] = [wq0, w_next(("uq", 2 * g + 1))]
                wuk, wq = ast["wuk"], ast["wq"]
                KHb, QNb, QRb = KH[h % 2], QN[h % 2], QR[h % 2]
                kcols = nkt * 128
                for k0 in range(0, kcols, 512):
                    kn = min(512, kcols - k0)
                    b = ps_lo()
                    for kc in range(4):
                        mm(psb(b, 0, kn), wv4(wuk, kc, hh * 128, hh * 128 + 128), CKV.v(kc, k0, k0 + kn), kc == 0, kc == 3)
                    act(KHb.v(0, k0, k0 + kn), psb(b, 0, kn), AF.Copy)
                if hh == 3:
                    w_issue()
                wqs = wq[hh // 2]
                qo = (hh % 2) * 256
                for (c0, n) in slabs:
                    b = ps_lo()
                    for kc in range(4):
                        mm(psb(b, 0, n), wv4(wqs, kc, qo, qo + 128), CQ.v(kc, c0, c0 + n), kc == 0, kc == 3)
                    act(QNb.v(0, c0, c0 + n), psb(b, 0, n), AF.Copy)
                    b2 = ps_lo()
                    for kc in range(4):
                        mm(psb(b2, 0, n), wv4(wqs, kc, qo + 128, qo + 256), CQ.v(kc, c0, c0 + n), kc == 0, kc == 3)
                    qraw = QRAW[h % 2].v(0, c0, c0 + n)
                    act(qraw, psb(b2, 0, n), AF.Copy)
                    tt(QRb.v(0, c0, c0 + n), qraw, CSL.v(0, c0, c0 + n), ALU.mult)
                    drain(-(-len(bgq) // max(1, att_slots[0])))
                    att_slots[0] -= 1
                if hh % 2 == 1:
                    w_issue()

            def scores(h):
                hh = h % 4
                KHb, QNb, QRb = KH[h % 2], QN[h % 2], QR[h % 2]
                for (c0, n) in slabs:
                    q0 = t0 + c0
                    nk = (q0 + n - 1) // 128 + 1
                    bo, bd = ps_od()
                    blocks = []
                    for kt in range(nk):
                        qs = max(q0, 128 * kt)
                        blocks.append((kt, qs - q0, n - (qs - q0), 128 * kt >= q0))

                    def qk(i):
                        kt, cs_, N, diag = blocks[i]
                        b = ps_lo()
                        mm(psb(b, 0, N), KHb.v(0, kt * 128, kt * 128 + 128), QNb.v(0, c0 + cs_, c0 + cs_ + N), True, False)
                        mm(psb(b, 0, N), KR.v(0, kt * 128, kt * 128 + 128), QRb.v(0, c0 + cs_, c0 + cs_ + N), False, not diag)
                        if diag:
                            w_ = min(128, N)
                            mm(psb(b, 0, w_), identb, MSK.v(0, 0, w_), False, True)
                        return b
                    sb = [qk(0)]
                    if len(blocks) > 1:
                        sb.append(qk(1))
                    for i, (kt, cs_, N, diag) in enumerate(blocks):
                        if i + 2 < len(blocks):
                            sb.append(qk(i + 2))
                        bs = sb[i]
                        pt = PT[i % 3].v(0, 0, N)
                        act(pt, psb(bs, 0, N), AF.Exp, scale=SCALE)
                        mm(psb(bo, cs_, cs_ + N), VG.v(kt, hh * 128, hh * 128 + 128), pt, i == 0, i == len(blocks) - 1, signal=True)
                        mm(psb(bd, cs_, cs_ + N), ones, pt, i == 0, i == len(blocks) - 1, signal=True)
                    rd = TMPS[rd_i[0] % 4].v(0, 0, n)
                    rd_i[0] += 1
                    act(rd, psb(bd, 0, n), AF.Copy)
                    act(B4.v(h, c0, c0 + n), psb(bo, 0, n), AF.Copy)
                    recip(rd, rd)
                    tt(B4.v(h, c0, c0 + n), B4.v(h, c0, c0 + n), rd, ALU.mult)
                    drain(-(-len(bgq) // max(1, att_slots[0])))
                    att_slots[0] -= 1

            prepare(0)
            for h in range(NH):
                pre = h + 1 < NH and (h + 1) % 4 != 0
                if pre:
                    prepare(h + 1)
                scores(h)
                if h + 1 < NH and not pre:
                    prepare(h + 1)
            drain()

            S.stage = "C2"
            for (c0, n) in slabs:
                (bm,) = ps_next(1)
                (bq,) = ps_next(1)
                for ch in range(16):
                    sq = TMPS[ch % 2]
                    sqv = R(sq.ap[:, 0, 0:n].bitcast(BF16)[:, 0:n], sq._keys(0, 1, 0, n))
                    act(sqv, B3.v(ch, c0, c0 + n), AF.Square)
                    mm(psb(bm, 0, n), ones, B3.v(ch, c0, c0 + n), ch == 0, ch == 15, signal=True)
                    mm(psb(bq, 0, n), ones, sqv, ch == 0, ch == 15, signal=True)
                mean = MEAN.v(0, c0, c0 + n)
                var = VAR.v(0, c0, c0 + n)
                rstd = RSTD.v(0, c0, c0 + n)
                tsc(mean, psb(bm, 0, n), 1.0 / D, None, ALU.mult)
                tt(var, mean, mean, ALU.mult)
                stt(var, psb(bq, 0, n), 1.0 / D, var, ALU.mult, ALU.subtract)
                act(rstd, var, AF.Sqrt, bias=eps_c, scale=1.0)
                recip(rstd, rstd)
                for ch in range(16):
                    t1 = TMPS[2 + ch % 2].v(0, 0, n)
                    tt(t1, B3.v(ch, c0, c0 + n), mean, ALU.subtract)
                    stt(t1, t1, vcol("g_ln", ch), rstd, ALU.mult, ALU.mult)
                    act(CB.v(ch, 32 + c0, 32 + c0 + n), t1, AF.Silu, bias=vcol("b_ln", ch))

            S.stage = "FE"
            for (wname, gname, bname, src_fn, first) in (
                    ("ao", "ga", "b_ga", lambda kc, c0, n: B4.v(kc, c0, c0 + n), True),
                    ("co", "gc", "b_gc", lambda kc, c0, n: CB.v(kc, 32 + c0, 32 + c0 + n), False)):
                for j in range(16):
                    wa = w_next((wname, j))
                    ba = ps_next(NS)
                    gemm16(wa, src_fn, slabs, ba)
                    w_issue()
                    wg = w_next(("in", gname, j))
                    bg = ps_next(NS)
                    gemm16(wg, u_rhs, slabs, bg)
                    w_issue()
                    for si, (c0, n) in enumerate(slabs):
                        sg = TMPS[si % 2].v(0, 0, n)
                        act(sg, psb(bg[si], 0, n), AF.Sigmoid, bias=vcol(bname, j))
                        if first:
                            tt(B3.v(j, c0, c0 + n), psb(ba[si], 0, n), sg, ALU.mult)
                        else:
                            t1 = TMPS[2 + si % 2].v(0, 0, n)
                            stt(t1, psb(ba[si], 0, n), vcol("b_co", j), sg, ALU.add, ALU.mult)
                            tt(B3.v(j, c0, c0 + n), B3.v(j, c0, c0 + n), t1, ALU.add)

            S.stage = "G"
            for j in range(nblk):
                rows = min(128, T - j * 128)
                xs = XS[xs_use[0] % 2]
                xsem = "x%d" % (xs_use[0] % 2)
                xs_use[0] += 1
                if rows < 128:
                    vmemset(xs.v(0), 0.0)
                dma("sp", xs.v(0, 0, D, 0, rows), R(h0[t0 + j * 128: t0 + j * 128 + rows, :]), xsem)
                for q4 in range(4):
                    (b,) = ps_next(1)
                    for c in range(4):
                        ch = q4 * 4 + c
                        tr(psb(b, c * 128, (c + 1) * 128), xs.v(0, ch * 128, (ch + 1) * 128), ident_f)
                    src = R(ps_t[:, b, :].rearrange("p (c w) -> p c w", c=4), [("ps", b)])
                    copy_any(H.v3(q4 * 4, q4 * 4 + 4, j * 128, (j + 1) * 128), src)
            for j in range(16):
                ws = w_next(("wo", j))
                banks = ps_next(NS)
                gemm16(ws, lambda kc, c0, n: B3.v(kc, c0, c0 + n), slabs, banks)
                w_issue()
                for si, (c0, n) in enumerate(slabs):
                    tt(H.v(j, c0, c0 + n), psb(banks[si], 0, n), H.v(j, c0, c0 + n), ALU.add)

            S.stage = "FFN"
            sumsq_rstd(lambda ch, c0, n: H.v(ch, c0, c0 + n), 16, slabs, float(D))
            for ch in range(16):
                for (c0, n) in slabs:
                    stt(B3.v(ch, c0, c0 + n), H.v(ch, c0, c0 + n), vcol("g_ffn", ch), RSTD.v(0, c0, c0 + n), ALU.mult, ALU.mult)

            def hn_rhs(kc, c0, n):
                return B3.v(kc, c0, c0 + n)
            for g in range(NFG):
                ag = ACTG[g % 2]
                for f in range(4):
                    wg = w_next(("fg", g * 4 + f))
                    bg = ps_next(NS)
                    gemm16(wg, hn_rhs, slabs, bg)
                    w_issue()
                    wu = w_next(("fu", g * 4 + f))
                    bu = ps_next(NS)
                    gemm16(wu, hn_rhs, slabs, bu)
                    w_issue()
                    for si, (c0, n) in enumerate(slabs):
                        sg = TMPS[si % 2].v(0, 0, n)
                        act(sg, psb(bg[si], 0, n), AF.Silu)
                        tt(ag.v(f, c0, c0 + n), psb(bu[si], 0, n), sg, ALU.mult)
                for cb in range(4):
                    wd = w_next(("fd", g, cb))
                    for o4 in range(4):
                        oc = cb * 4 + o4
                        banks = ps_next(NS)
                        for kc in range(4):
                            for si, (c0, n) in enumerate(slabs):
                                mm(psb(banks[si], 0, n), wv4(wd, kc, o4 * 128, o4 * 128 + 128), ag.v(kc, c0, c0 + n), kc == 0, kc == 3)
                        for si, (c0, n) in enumerate(slabs):
                            tt(H.v(oc, c0, c0 + n), psb(banks[si], 0, n), H.v(oc, c0, c0 + n), ALU.add)
                    w_issue()

            S.stage = "OUT"
            sumsq_rstd(lambda ch, c0, n: H.v(ch, c0, c0 + n), 16, slabs, float(D))
            for ch in range(16):
                for (c0, n) in slabs:
                    stt(H.v(ch, c0, c0 + n), H.v(ch, c0, c0 + n), vcol("g_fin", ch), RSTD.v(0, c0, c0 + n), ALU.mult, ALU.mult)
            for j in range(nblk):
                rows = min(128, T - j * 128)
                xs = XS[xs_use[0] % 2]
                xsem = "x%d" % (xs_use[0] % 2)
                xs_use[0] += 1
                for q4 in range(4):
                    (b,) = ps_next(1)
                    for c in range(4):
                        ch = q4 * 4 + c
                        tr(psb(b, c * 128, (c + 1) * 128), H.v(ch, j * 128, (j + 1) * 128), ident_f)
                    copy_any(xs.v(0, q4 * 512, (q4 + 1) * 512), psb(b))
                tok0 = t0 + j * 128
                p0 = max(0, NMETA - tok0)
                if rows > p0:
                    dma("sp", R(out_d[tok0 + p0 - NMETA: tok0 + rows - NMETA, :]), xs.v(0, 0, D, p0, rows), xsem)

        assert wk["k"] == total_blocks and wk["issued"] == total_blocks, (wk, total_blocks)

        dma_sems = sorted(S.dma_count.keys())
        sem_names = list(S.ops.keys()) + dma_sems
        semh = {n: es.enter_context(nc.semaphore("s_" + n)) for n in sem_names}
        for k, v in S.maxwait.items():
            lim = S.count[k] if k in S.count else S.dma_count[k]
            assert v <= lim, ("wait beyond final count", k, v, lim)
        block = es.enter_context(nc.Block())

        def run(e, name):
            for fn, waits, signal, dsem in S.ops[name]:
                for (k, v) in waits:
                    e.wait_ge(semh[k], v)
                ins = fn(e)
                if signal:
                    ins.then_inc(semh[name], 1)
                if dsem is not None:
                    ins.then_inc(semh[dsem], 16)
            if name == "sp":
                for k in dma_sems:
                    if k.startswith("x") or k.startswith("c"):
                        e.wait_ge(semh[k], S.dma_count[k])

        @block.tensor
        def _(e):
            run(e, "pe")

        @block.scalar
        def _(e):
            run(e, "act")

        @block.vector
        def _(e):
            run(e, "dve")

        @block.gpsimd
        def _(e):
            run(e, "pool")

        @block.sync
        def _(e):
            run(e, "sp")
    stats = {n: len(v) for n, v in S.ops.items()}
    stats['tags'] = S.tags
    return nc, stats


_CACHE = {}


def kernel(x, meta_tokens, g_mix, w_in, b_glu, b_gate, w_dw, b_dw, g_conv_ln, b_conv_ln, w_conv_out, b_conv_out,
           g_q_lora, w_uq, g_kv_lora, w_uk, w_uv, w_attn_out, w_out, g_ffn, w_ffn_gate, w_ffn_up, w_ffn_down, g_final):
    f = lambda a: np.asarray(a, np.float32)
    x = f(x)
    meta = f(meta_tokens)
    wst = pack_weights(f(w_in)[0], f(w_conv_out)[0], f(w_uq)[0], f(w_uk)[0], f(w_uv)[0], f(w_attn_out)[0], f(w_out)[0],
                       f(w_ffn_gate)[0], f(w_ffn_up)[0], f(w_ffn_down)[0])
    vecs = pack_vecs(f(b_glu)[0], f(b_gate)[0], f(b_dw)[0], f(g_conv_ln)[0], f(b_conv_ln)[0], f(b_conv_out)[0],
                     f(g_q_lora)[0], f(g_kv_lora)[0], f(g_ffn)[0], f(g_final), f(w_dw)[0])
    consts, cs = const_tables()
    gmix = np.ascontiguousarray(f(g_mix)[0])
    if "nc" not in _CACHE:
        _CACHE["nc"] = build_program()[0]
    nc = _CACHE["nc"]
    nb = x.shape[0]
    in_maps = []
    for b in range(nb):
        h0 = np.ascontiguousarray(np.concatenate([meta, x[b]], axis=0))
        in_maps.append({"h0": h0, "wst": wst, "vecs": vecs, "consts": consts, "cs": cs, "gmix": gmix})
    res = run_bass_kernel_spmd(nc, in_maps, core_ids=list(range(nb)))
    return np.stack([np.asarray(r["out"], np.float32) for r in res.results], axis=0)
```

```python
import numpy as np
import concourse.bass as bass
import concourse.mybir as mybir
from concourse.bass_utils import run_bass_kernel_spmd

F32, BF16 = mybir.dt.float32, mybir.dt.bfloat16
AF = mybir.ActivationFunctionType
ALU = mybir.AluOpType

D = 2048
NCH = 16
NMETA = 16
SEQ = 2048
L = NMETA + SEQ
LP = 2176
TILES = [(0, 768), (768, 768), (1536, 528)]
SL = 384
TB = 768
DFF = 5632
NFG = 11
NH = 16
SCALE = float(192 ** -0.5)
EPS = 1e-6
import os
NO_OUT_OVERLAP = bool(int(os.environ.get('NO_OUT_OVERLAP', '0')))
NO_A_OVERLAP = bool(int(os.environ.get('NO_A_OVERLAP', '0')))
NW = 4
CONV_F_SHARE = (10, 3, 2)
CONV_DRAIN_B2 = 5
GR = 256


def stream_order():
    o = []
    for j in range(4):
        o.append(("in", "cq", j))
    for j in range(4):
        o.append(("in", "ckv", j))
    o.append(("in", "kr", 0))
    for j in range(16):
        o.append(("in", "za", j))
        o.append(("in", "zb", j))
    for g in range(4):
        o.append(("uv", g))
        o.append(("uq", 2 * g))
        o.append(("uk", g))
        o.append(("uq", 2 * g + 1))
    for j in range(16):
        o.append(("ao", j))
        o.append(("in", "ga", j))
    for j in range(16):
        o.append(("co", j))
        o.append(("in", "gc", j))
    for j in range(16):
        o.append(("wo", j))
    for g in range(NFG):
        for f in range(4):
            o.append(("fg", g * 4 + f))
            o.append(("fu", g * 4 + f))
        for cb in range(4):
            o.append(("fd", g, cb))
    return o


ORDER = stream_order()
NB = len(ORDER)

VOFF = {}
_c = 0
for _n, _w in [("b_za", 16), ("b_zb", 16), ("b_gc", 16), ("b_ga", 16), ("b_dw", 16), ("g_ln", 16), ("b_ln", 16),
               ("b_co", 16), ("g_q", 4), ("g_kv", 4), ("g_ffn", 16), ("g_fin", 16), ("eps", 1), ("w_dw", 31 * 16)]:
    VOFF[_n] = _c
    _c += _w
NV = _c


def _pack_k2048(W, cols):
    return np.ascontiguousarray(W[:, cols].reshape(16, 128, 128).transpose(1, 0, 2).reshape(128, 2048))


def _pack_k512(W, cols):
    return np.ascontiguousarray(W[:, cols].reshape(4, 128, 512).transpose(1, 0, 2).reshape(128, 2048))


def pack_weights(w_in, w_conv_out, w_uq, w_uk, w_uv, w_attn_out, w_out, w_ffn_gate, w_ffn_up, w_ffn_down):
    wst = np.empty((NB, 128, 2048), np.float32)
    ar = np.arange
    base = {"za": 0, "zb": 2048, "cq": 4096, "ckv": 4608, "gc": 5184, "ga": 7232}
    for i, d in enumerate(ORDER):
        t = d[0]
        if t == "in":
            if d[1] == "kr":
                r = 5120 + ar(64)
                sw = 5120 + np.concatenate([ar(32, 64), ar(0, 32)])
                cols = np.concatenate([r, sw])
            else:
                cols = base[d[1]] + d[2] * 128 + ar(128)
            wst[i] = _pack_k2048(w_in, cols)
        elif t == "co":
            wst[i] = _pack_k2048(w_conv_out, d[1] * 128 + ar(128))
        elif t == "ao":
            wst[i] = _pack_k2048(w_attn_out, d[1] * 128 + ar(128))
        elif t == "wo":
            wst[i] = _pack_k2048(w_out, d[1] * 128 + ar(128))
        elif t == "fg":
            wst[i] = _pack_k2048(w_ffn_gate, d[1] * 128 + ar(128))
        elif t == "fu":
            wst[i] = _pack_k2048(w_ffn_up, d[1] * 128 + ar(128))
        elif t == "fd":
            g, cb = d[1], d[2]
            blk = w_ffn_down[g * 512:(g + 1) * 512, cb * 512:(cb + 1) * 512]
            wst[i] = blk.reshape(4, 128, 512).transpose(1, 0, 2).reshape(128, 2048)
        elif t == "uv":
            wst[i] = _pack_k512(w_uv, d[1] * 512 + ar(512))
        elif t == "uk":
            wst[i] = _pack_k512(w_uk, d[1] * 512 + ar(512))
        elif t == "uq":
            cols = []
            for h in (2 * d[1], 2 * d[1] + 1):
                b = h * 192
                cols += [b + ar(128), b + 128 + ar(64), b + 128 + ar(32, 64), b + 128 + ar(0, 32)]
            wst[i] = _pack_k512(w_uq, np.concatenate(cols))
        else:
            raise ValueError(t)
    return wst


def _col(v):
    return np.ascontiguousarray(np.asarray(v, np.float32).reshape(-1, 128).T)


def pack_vecs(b_glu, b_gate, b_dw, g_conv_ln, b_conv_ln, b_conv_out, g_q_lora, g_kv_lora, g_ffn, g_final, w_dw):
    v = np.zeros((128, NV), np.float32)

    def put(name, arr):
        c = _col(arr)
        v[:, VOFF[name]:VOFF[name] + c.shape[1]] = c
    put("b_za", b_glu[:2048])
    put("b_zb", b_glu[2048:])
    put("b_gc", b_gate[:2048])
    put("b_ga", b_gate[2048:])
    put("b_dw", b_dw)
    put("g_ln", g_conv_ln)
    put("b_ln", b_conv_ln)
    put("b_co", b_conv_out)
    put("g_q", g_q_lora)
    put("g_kv", g_kv_lora)
    put("g_ffn", g_ffn)
    put("g_fin", g_final)
    v[:, VOFF["eps"]] = EPS
    for j in range(31):
        c = _col(w_dw[j])
        v[:, VOFF["w_dw"] + j * 16: VOFF["w_dw"] + (j + 1) * 16] = c
    return v


def const_tables():
    p = np.arange(128)
    ident = np.eye(128, dtype=np.float32)
    maskneg = np.where(p[:, None] <= p[None, :], 0.0, -30000.0).astype(np.float32)
    fold = (p[:, None] % 64 == p[None, :] % 64).astype(np.float32)
    consts = np.concatenate([ident, maskneg, fold], axis=1)
    pos = np.arange(LP, dtype=np.float32)
    inv_freq = (np.float32(10000.0) ** (-np.arange(0, 64, 2, dtype=np.float32) / np.float32(64))).astype(np.float32)
    ang = (pos[None, :] * inv_freq[:, None]).astype(np.float32)
    cos, sin = np.cos(ang).astype(np.float32), np.sin(ang).astype(np.float32)
    cs = np.concatenate([cos, cos, -sin, sin], axis=0).astype(np.float32)
    return np.ascontiguousarray(consts), np.ascontiguousarray(cs)


class R:
    __slots__ = ("ap", "keys")

    def __init__(self, ap, keys=()):
        self.ap = ap
        self.keys = frozenset(keys)


class Sched:
    def __init__(self):
        self.ops = {n: [] for n in ("pe", "act", "dve", "pool", "sp")}
        self.count = {n: 0 for n in self.ops}
        self.waited = {n: {} for n in self.ops}
        self.last_w = {}
        self.readers = {}
        self.dma_count = {}
        self.maxwait = {}
        self.stage = ''
        self.tags = {n: [] for n in self.ops}

    def op(self, eng, fn, reads=(), writes=(), signal=True, dma_sem=None):
        deps = {}

        def need(ev):
            if ev is not None and deps.get(ev[0], 0) < ev[1]:
                deps[ev[0]] = ev[1]
        rk = set()
        for r in reads:
            rk |= r.keys
        wk = set()
        for w in writes:
            wk |= w.keys
        for k in rk:
            need(self.last_w.get(k))
        for k in wk:
            need(self.last_w.get(k))
            rd = self.readers.get(k)
            if rd:
                for s, v in rd.items():
                    need((s, v))
        if dma_sem is not None:
            self.dma_count[dma_sem] = self.dma_count.get(dma_sem, 0) + 16
            ev = (dma_sem, self.dma_count[dma_sem])
            signal = False
        elif signal:
            self.count[eng] += 1
            ev = (eng, self.count[eng])
        else:
            ev = (eng, self.count[eng] + 1)
        waits = []
        wd = self.waited[eng]
        for k, v in deps.items():
            if k == eng:
                continue
            if wd.get(k, 0) >= v:
                continue
            wd[k] = v
            waits.append((k, v))
            if self.maxwait.get(k, 0) < v:
                self.maxwait[k] = v
        self.ops[eng].append((fn, waits, signal, dma_sem))
        self.tags[eng].append(self.stage)
        for k in rk:
            d = self.readers.setdefault(k, {})
            if d.get(ev[0], 0) < ev[1]:
                d[ev[0]] = ev[1]
        for k in wk:
            self.last_w[k] = ev
            self.readers[k] = {}
        return ev


class Buf:
    def __init__(self, arena, off, dtype, n0, n1, tag):
        self.off, self.dtype, self.n0, self.n1, self.tag = off, dtype, n0, n1, tag
        self.es = 4 if dtype == F32 else 2
        nb = n0 * n1 * self.es
        a = arena[:, off // 2: (off + nb) // 2]
        if dtype == F32:
            a = a.bitcast(F32)
        self.ap = a.rearrange("p (a b) -> p a b", a=n0)

    def _keys(self, i0, i1, c0, c1):
        ks = set()
        for i in range(i0, i1):
            lo = self.off + (i * self.n1 + c0) * self.es
            hi = self.off + (i * self.n1 + c1) * self.es
            ks.update(("sb", g) for g in range(lo // GR, (hi - 1) // GR + 1))
        return ks

    def v(self, i, c0=0, c1=None, p0=0, p1=128):
        c1 = self.n1 if c1 is None else c1
        return R(self.ap[p0:p1, i, c0:c1], self._keys(i, i + 1, c0, c1))

    def v3(self, i0, i1, c0=0, c1=None):
        c1 = self.n1 if c1 is None else c1
        return R(self.ap[:, i0:i1, c0:c1], self._keys(i0, i1, c0, c1))


def build_program(ntiles=3, dbg=None):
    nc = bass.Bass("TRN2", target_bir_lowering=False)
    h0 = nc.dram_tensor("h0", [L, D], F32, kind="ExternalInput").ap()
    wst = nc.dram_tensor("wst", [NB, 128, 2048], F32, kind="ExternalInput").ap()
    vecs_d = nc.dram_tensor("vecs", [128, NV], F32, kind="ExternalInput").ap()
    consts_d = nc.dram_tensor("consts", [128, 384], F32, kind="ExternalInput").ap()
    cs_d = nc.dram_tensor("cs", [128, LP], F32, kind="ExternalInput").ap()
    gmix_d = nc.dram_tensor("gmix", [D], F32, kind="ExternalInput").ap()
    out_d = nc.dram_tensor("out", [SEQ, D], F32, kind="ExternalOutput").ap()
    dbg_d = {}
    if dbg:
        for name, shape in dbg.items():
            dbg_d[name] = nc.dram_tensor("dbg_" + name, list(shape), BF16, kind="ExternalOutput").ap()

    tiles = TILES[:ntiles]
    S = Sched()
    ARENA_BYTES = 207 * 1024 + 512
    import contextlib
    with contextlib.ExitStack() as es:
        arena = es.enter_context(nc.sbuf_tensor("arena", [128, ARENA_BYTES // 2], BF16))
        ps_t = es.enter_context(nc.psum_tensor("ps", [128, 8, 512], F32))
        cur = [0]

        def alloc(nbytes):
            o = cur[0]
            cur[0] = (o + nbytes + GR - 1) // GR * GR
            assert cur[0] <= ARENA_BYTES, ("SBUF overflow", cur[0])
            return o

        def mk(dtype, n0, n1, tag, off=None):
            es_ = 4 if dtype == F32 else 2
            if off is None:
                off = alloc(n0 * n1 * es_)
            return Buf(arena, off, dtype, n0, n1, tag)

        B1 = mk(BF16, NCH, TB, "B1")
        CBW = 32 + TB
        o_b4 = alloc(NCH * TB * 2 + NCH * CBW * 2)
        B4 = mk(BF16, NCH, TB, "B4", o_b4)
        CB = mk(BF16, NCH, CBW, "CB", o_b4 + NCH * TB * 2)
        H = mk(F32, NCH, TB, "H", o_b4)
        HALO = mk(BF16, NCH, 32, "HALO")
        B3 = mk(BF16, NCH, TB, "B3")
        CQ = mk(BF16, 4, TB, "CQ")
        CKV = mk(BF16, 4, LP, "CKV")
        KR = mk(BF16, 1, LP, "KR")
        o_sa = alloc(36352)
        XS = [mk(F32, 1, D, "XS%d" % i, o_sa + i * 8192) for i in range(2)]
        UTM = mk(BF16, 1, D, "UTM", o_sa + 16384)
        JUNK = mk(BF16, 1, D, "JUNK", o_sa + 20480)
        GMX = mk(F32, 1, D, "GMX", o_sa + 24576)
        VG = mk(BF16, 17, 512, "VG", o_sa)
        KH = [mk(BF16, 1, LP, "KH%d" % i, o_sa + 17408 + i * 4352) for i in range(2)]
        QN = [mk(BF16, 1, TB, "QN%d" % i, o_sa + 26112 + i * 1536) for i in range(2)]
        QR = [mk(BF16, 1, TB, "QR%d" % i, o_sa + 29184 + i * 1536) for i in range(2)]
        PT = [mk(BF16, 1, SL, "PT%d" % i, o_sa + 32256 + i * 768) for i in range(3)]
        RDEN = mk(F32, 1, SL, "RDEN", o_sa + 34560)
        assert 34560 + 1536 <= 36352
        CACC = [mk(F32, 1, TB, "CACC%d" % i) for i in range(2)]
        ACTG = [mk(BF16, 4, TB, "ACTG0", CQ.off), mk(BF16, 4, TB, "ACTG1", CACC[0].off)]
        assert CACC[1].off == CACC[0].off + 3072
        TMPS = [mk(F32, 1, SL, "TMP%d" % i) for i in range(4)]
        MEAN = mk(F32, 1, TB, "MEAN")
        RSTD = mk(F32, 1, TB, "RSTD")
        VAR = mk(F32, 1, TB, "VAR")
        QRAW = [mk(BF16, 1, TB, "QRAW0", MEAN.off), mk(BF16, 1, TB, "QRAW1", VAR.off)]
        WS = [mk(BF16, 16, 128, "WS%d" % i) for i in range(NW)]
        VEC = mk(F32, 1, NV, "VEC")
        CSL = mk(F32, 1, TB, "CSL")
        CST = mk(F32, 1, 384, "CST")
        IDB = mk(BF16, 1, 128, "IDB")
        MSK = mk(BF16, 1, 128, "MSK")
        FLD = mk(BF16, 1, 128, "FLD")
        ONE = mk(BF16, 1, 128, "ONE")
        SSB = mk(F32, 1, 8, "SSB")

        def psb(b, c0=0, c1=512):
            return R(ps_t[:, b, c0:c1], [("ps", b)])

        def psb_bf(b):
            return R(ps_t[:, b, :].bitcast(BF16), [("ps", b)])

        def vcol(name, j=0):
            c = VOFF[name] + j
            return R(VEC.ap[:, 0, c:c + 1], VEC._keys(0, 1, c, c + 1))

        def mm(out, lhsT, rhs, start, stop, signal=None):
            S.op("pe", lambda e, o=out.ap, l=lhsT.ap, r=rhs.ap, a=start, b=stop: e.matmul(o, lhsT=l, rhs=r, start=a, stop=b),
                 reads=[lhsT, rhs], writes=[out], signal=(stop if signal is None else signal))

        def tr(out, in_, ident):
            S.op("pe", lambda e, o=out.ap, i=in_.ap, d=ident.ap: e.transpose(o, i, d), reads=[in_, ident], writes=[out])

        def act(out, in_, func, bias=None, scale=None, accum=None, extra_reads=()):
            kw = {}
            rd = [in_] + list(extra_reads)
            wr = [out]
            if bias is not None:
                kw["bias"] = bias.ap
                rd.append(bias)
            if scale is not None:
                if isinstance(scale, R):
                    kw["scale"] = scale.ap
                    rd.append(scale)
                else:
                    kw["scale"] = float(scale)
            if accum is not None:
                kw["accum_out"] = accum.ap
                wr.append(accum)
            S.op("act", lambda e, o=out.ap, i=in_.ap, f=func, kw=kw: e.activation(out=o, in_=i, func=f, **kw), reads=rd, writes=wr)

        def _sc(x, rd):
            if isinstance(x, R):
                rd.append(x)
                return x.ap
            return x

        def tsc(out, in0, s1, s2, op0, op1=None):
            rd = [in0]
            a1 = _sc(s1, rd)
            a2 = _sc(s2, rd) if s2 is not None else None
            if op1 is None:
                S.op("dve", lambda e, o=out.ap, i=in0.ap: e.tensor_scalar(out=o, in0=i, scalar1=a1, scalar2=None, op0=op0), reads=rd, writes=[out])
            else:
                S.op("dve", lambda e, o=out.ap, i=in0.ap: e.tensor_scalar(out=o, in0=i, scalar1=a1, scalar2=a2, op0=op0, op1=op1), reads=rd, writes=[out])

        def stt(out, in0, scalar, in1, op0, op1):
            rd = [in0, in1]
            a = _sc(scalar, rd)
            S.op("dve", lambda e, o=out.ap, i0=in0.ap, i1=in1.ap: e.scalar_tensor_tensor(out=o, in0=i0, scalar=a, in1=i1, op0=op0, op1=op1),
                 reads=rd, writes=[out])

        def tt(out, in0, in1, op):
            S.op("dve", lambda e, o=out.ap, i0=in0.ap, i1=in1.ap: e.tensor_tensor(out=o, in0=i0, in1=i1, op=op), reads=[in0, in1], writes=[out])

        def recip(out, in_):
            S.op("dve", lambda e, o=out.ap, i=in_.ap: e.reciprocal(out=o, in_=i), reads=[in_], writes=[out])

        def vcopy(out, in_):
            S.op("dve", lambda e, o=out.ap, i=in_.ap: e.tensor_copy(out=o, in_=i), reads=[in_], writes=[out])

        def vmemset(out, val):
            S.op("dve", lambda e, o=out.ap: e.memset(o, val), writes=[out])

        def dma(eng, out, in_, sem):
            S.op(eng, lambda e, o=out.ap, i=in_.ap: e.dma_start(out=o, in_=i), reads=[in_], writes=[out], dma_sem=sem)

        cp_flip = [0]

        def copy_any(out, in_):
            cp_flip[0] ^= 1
            if cp_flip[0]:
                act(out, in_, AF.Copy)
            else:
                vcopy(out, in_)

        import collections
        bgq = collections.deque()

        def drain(n=None):
            k = len(bgq) if n is None else min(n, len(bgq))
            for _ in range(k):
                bgq.popleft()()

        wk = {"issued": 0, "k": 0}
        total_blocks = NB * len(tiles)

        def w_issue():
            k = wk["issued"]
            if k >= total_blocks:
                return
            wk["issued"] += 1
            s = k % NW
            dst = R(WS[s].ap.rearrange("p a b -> p (a b)"), WS[s]._keys(0, 16, 0, 128))
            dma("pool", dst, R(wst[k % NB]), "w%d" % s)

        def w_next(desc):
            k = wk["k"]
            assert ORDER[k % NB] == desc, (ORDER[k % NB], desc)
            wk["k"] += 1
            return WS[k % NW]

        def wv16(ws, kc):
            return ws.v(kc)

        def wv4(ws, kc, c0, c1):
            ap = ws.ap.rearrange("p a b -> p (a b)")[:, kc * 512 + c0: kc * 512 + c1]
            return R(ap, ws._keys(0, 16, 0, 128))

        psr = {"i": 0, "lo": 0, "od": 0}

        def ps_next(n=1):
            b = psr["i"]
            if b + n > 8:
                b = 0
            psr["i"] = (b + n) % 8
            return list(range(b, b + n))

        def ps_lo():
            b = psr["lo"]
            psr["lo"] = (b + 1) % 4
            return b

        def ps_od():
            b = 4 + 2 * psr["od"]
            psr["od"] ^= 1
            return b, b + 1

        dma("sp", VEC.v(0), R(vecs_d), "c0")
        dma("sp", CST.v(0), R(consts_d), "c1")
        for i in range(NW):
            w_issue()
        ident_f = CST.v(0, 0, 128)
        act(IDB.v(0), CST.v(0, 0, 128), AF.Copy)
        act(MSK.v(0), CST.v(0, 128, 256), AF.Copy)
        act(FLD.v(0), CST.v(0, 256, 384), AF.Copy)
        vmemset(ONE.v(0), 1.0)
        vmemset(CKV.v3(0, 4), 0.0)
        vmemset(KR.v(0), 0.0)
        identb, maskb, foldb, ones = IDB.v(0), MSK.v(0), FLD.v(0), ONE.v(0)
        eps_c = vcol("eps")

        xs_use = [0]

        def sumsq_rstd(src_fn, nch, slabs, dim, tmp_i=0, RS=None):
            RS = RSTD if RS is None else RS
            for (c0, n) in slabs:
                (b,) = ps_next(1)
                for ch in range(nch):
                    sq = TMPS[(tmp_i + ch) % 2]
                    sqv = R(sq.ap[:, 0, 0:n].bitcast(BF16)[:, 0:n], sq._keys(0, 1, 0, n))
                    act(sqv, src_fn(ch, c0, n), AF.Square)
                    mm(psb(b, 0, n), ones, sqv, ch == 0, ch == nch - 1, signal=True)
                act(RS.v(0, c0, c0 + n), psb(b, 0, n), AF.Sqrt, bias=eps_c, scale=1.0 / dim)
                recip(RS.v(0, c0, c0 + n), RS.v(0, c0, c0 + n))

        def gemm16(ws, rhs_fn, slabs, banks):
            for kc in range(16):
                for si, (c0, n) in enumerate(slabs):
                    mm(psb(banks[si], 0, n), wv16(ws, kc), rhs_fn(kc, c0, n), kc == 0, kc == 15)

        def tile_params(ti):
            t0, T = tiles[ti]
            slabs = [(c, min(SL, T - c)) for c in range(0, T, SL)]
            return t0, T, slabs, (T + 127) // 128

        def u_rhs(kc, c0, n):
            return B1.v(kc, c0, c0 + n)

        def make_A(ti):
            t0, T, slabs, nblk = tile_params(ti)
            parts = []

            def p_load():
                dma("sp", GMX.v(0), R(gmix_d.partition_broadcast(128)), "c2")
                dma("sp", CSL.v(0, 0, T), R(cs_d[:, t0:t0 + T]), "c3")
            parts.append(p_load)
            for j in range(nblk):
                def p1(j=j):
                    S.stage = "A"
                    rows = min(128, T - j * 128)
                    xs = XS[xs_use[0] % 2]
                    xsem = "x%d" % (xs_use[0] % 2)
                    xs_use[0] += 1
                    if rows < 128:
                        vmemset(xs.v(0), 0.0)
                    dma("sp", xs.v(0, 0, D, 0, rows), R(h0[t0 + j * 128: t0 + j * 128 + rows, :]), xsem)
                    ss = SSB.v(0, 0, 1)
                    vmemset(ss, 0.0)
                    act(JUNK.v(0), xs.v(0), AF.Square, accum=ss)
                    act(SSB.v(0, 1, 2), ss, AF.Sqrt, bias=eps_c, scale=1.0 / D)
                    recip(SSB.v(0, 2, 3), SSB.v(0, 1, 2))
                    stt(UTM.v(0), xs.v(0), SSB.v(0, 2, 3), GMX.v(0), ALU.mult, ALU.mult)
                    for half in range(2):
                        (b,) = ps_next(1)
                        pb = psb_bf(b)
                        for c in range(8):
                            ch = half * 8 + c
                            tr(R(pb.ap[:, c * 128:(c + 1) * 128], pb.keys), UTM.v(0, ch * 128, (ch + 1) * 128), identb)
                        src = R(pb.ap.rearrange("p (c w) -> p c w", c=8), pb.keys)
                        copy_any(B1.v3(half * 8, half * 8 + 8, j * 128, (j + 1) * 128), src)
                parts.append(p1)
            return parts

        def make_OUT(ti):
            t0, T, slabs, nblk = tile_params(ti)
            parts = []
            for sl_ in slabs:
                def p_stat(sl_=sl_):
                    S.stage = "OUT"
                    sumsq_rstd(lambda ch, c0, n: H.v(ch, c0, c0 + n), 16, [sl_], float(D), RS=VAR)
                    c0, n = sl_
                    for ch in range(16):
                        stt(H.v(ch, c0, c0 + n), H.v(ch, c0, c0 + n), vcol("g_fin", ch), VAR.v(0, c0, c0 + n), ALU.mult, ALU.mult)
                parts.append(p_stat)
            for j in range(nblk):
                def p_blk(j=j):
                    S.stage = "OUT"
                    rows = min(128, T - j * 128)
                    xs = XS[xs_use[0] % 2]
                    xsem = "x%d" % (xs_use[0] % 2)
                    xs_use[0] += 1
                    for q4 in range(4):
                        (b,) = ps_next(1)
                        for c in range(4):
                            ch = q4 * 4 + c
                            tr(psb(b, c * 128, (c + 1) * 128), H.v(ch, j * 128, (j + 1) * 128), ident_f)
                        copy_any(xs.v(0, q4 * 512, (q4 + 1) * 512), psb(b))
                    tok0 = t0 + j * 128
                    p0 = max(0, NMETA - tok0)
                    if rows > p0:
                        dma("sp", R(out_d[tok0 + p0 - NMETA: tok0 + rows - NMETA, :]), xs.v(0, 0, D, p0, rows), xsem)
                parts.append(p_blk)
            return parts

        for p in make_A(0):
            p()
        pend_out = []
        for ti in range(len(tiles)):
            t0, T, slabs, nblk = tile_params(ti)
            NS = len(slabs)

            def side():
                if pend_out:
                    pend_out.pop(0)()

            S.stage = "B1"
            for name, dst, gname, col0 in (("cq", CQ, "g_q", 0), ("ckv", CKV, "g_kv", t0)):
                for j in range(4):
                    ws = w_next(("in", name, j))
                    banks = ps_next(NS)
                    gemm16(ws, u_rhs, slabs, banks)
                    w_issue()
                    for si, (c0, n) in enumerate(slabs):
                        copy_any(dst.v(j, col0 + c0, col0 + c0 + n), psb(banks[si], 0, n))
                    side()
                    S.stage = "B1"
                sumsq_rstd(lambda ch, c0, n, dst=dst, col0=col0: dst.v(ch, col0 + c0, col0 + c0 + n), 4, slabs, 512.0)
                for j in range(4):
                    for (c0, n) in slabs:
                        d_ = dst.v(j, col0 + c0, col0 + c0 + n)
                        stt(d_, d_, vcol(gname, j), RSTD.v(0, c0, c0 + n), ALU.mult, ALU.mult)
            ws = w_next(("in", "kr", 0))
            banks = ps_next(NS)
            gemm16(ws, u_rhs, slabs, banks)
            w_issue()
            for si, (c0, n) in enumerate(slabs):
                tmp = TMPS[2 + si % 2]
                tv = R(tmp.ap[:, 0, 0:n].bitcast(BF16)[:, 0:n], tmp._keys(0, 1, 0, n))
                tt(tv, psb(banks[si], 0, n), CSL.v(0, c0, c0 + n), ALU.mult)
                (b,) = ps_next(1)
                mm(psb(b, 0, n), foldb, tv, True, True)
                copy_any(KR.v(0, t0 + c0, t0 + c0 + n), psb(b, 0, n))
            while pend_out:
                side()

            S.stage = "B2"
            for j in range(16):
                wa = w_next(("in", "za", j))
                ba = ps_next(NS)
                gemm16(wa, u_rhs, slabs, ba)
                w_issue()
                drain(CONV_DRAIN_B2)
                wb = w_next(("in", "zb", j))
                bb = ps_next(NS)
                gemm16(wb, u_rhs, slabs, bb)
                w_issue()
                for si, (c0, n) in enumerate(slabs):
                    sg = TMPS[si % 2].v(0, 0, n)
                    act(sg, psb(bb[si], 0, n), AF.Sigmoid, bias=vcol("b_zb", j))
                    stt(CB.v(j, 32 + c0, 32 + c0 + n), psb(ba[si], 0, n), vcol("b_za", j), sg, ALU.add, ALU.mult)
                acc = CACC[j % 2].v(0, 0, T)
                if ti == 0:
                    bgq.append(lambda ch=j: vmemset(CB.v(ch, 0, 32), 0.0))
                else:
                    bgq.append(lambda ch=j: vcopy(CB.v(ch, 0, 32), HALO.v(ch, 0, 32)))
                bgq.append(lambda ch=j: vcopy(HALO.v(ch, 0, 32), CB.v(ch, T, T + 32)))
                bgq.append(lambda acc=acc, ch=j: tsc(acc, CB.v(ch, 2, 2 + T), vcol("w_dw", ch), vcol("b_dw", ch), ALU.mult, ALU.add))
                for jj in range(1, 31):
                    dst = acc if jj < 30 else CB.v(j, 32, 32 + T)
                    bgq.append(lambda acc=acc, ch=j, jj=jj, dst=dst: stt(dst, CB.v(ch, 2 + jj, 2 + jj + T), vcol("w_dw", jj * 16 + ch), acc, ALU.mult, ALU.add))
                drain(CONV_DRAIN_B2)
            att_slots = [16 * NS * 2 + 16 * CONV_F_SHARE[min(ti, 2)]]

            S.stage = "D"
            kend = t0 + T
            nkt = (kend + 127) // 128
            ast = {}
            rd_i = [0]

            def prepare(h):
                g, hh = divmod(h, 4)
                if hh == 0:
                    wuv = w_next(("uv", g))
                    for kt in range(nkt):
                        b = ps_lo()
                        for kc in range(4):
                            mm(psb(b), CKV.v(kc, kt * 128, kt * 128 + 128), wv4(wuv, kc, 0, 512), kc == 0, kc == 3)
                        act(VG.v(kt), psb(b), AF.Copy)
                    w_issue()
                    wq0 = w_next(("uq", 2 * g))
                    ast["wuk"] = w_next(("uk", g))
                    ast["wq"] = [wq0, w_next(("uq", 2 * g + 1))]
                wuk, wq = ast["wuk"], ast["wq"]
                KHb, QNb, QRb = KH[h % 2], QN[h % 2], QR[h % 2]
                kcols = nkt * 128
                for k0 in range(0, kcols, 512):
                    kn = min(512, kcols - k0)
                    b = ps_lo()
                    for kc in range(4):
                        mm(psb(b, 0, kn), wv4(wuk, kc, hh * 128, hh * 128 + 128), CKV.v(kc, k0, k0 + kn), kc == 0, kc == 3)
                    act(KHb.v(0, k0, k0 + kn), psb(b, 0, kn), AF.Copy)
                if hh == 3:
                    w_issue()
                wqs = wq[hh // 2]
                qo = (hh % 2) * 256
                for (c0, n) in slabs:
                    b = ps_lo()
                    for kc in range(4):
                        mm(psb(b, 0, n), wv4(wqs, kc, qo, qo + 128), CQ.v(kc, c0, c0 + n), kc == 0, kc == 3)
                    act(QNb.v(0, c0, c0 + n), psb(b, 0, n), AF.Copy)
                    b2 = ps_lo()
                    for kc in range(4):
                        mm(psb(b2, 0, n), wv4(wqs, kc, qo + 128, qo + 256), CQ.v(kc, c0, c0 + n), kc == 0, kc == 3)
                    qraw = QRAW[h % 2].v(0, c0, c0 + n)
                    act(qraw, psb(b2, 0, n), AF.Copy)
                    tt(QRb.v(0, c0, c0 + n), qraw, CSL.v(0, c0, c0 + n), ALU.mult)
                    drain(-(-len(bgq) // max(1, att_slots[0])))
                    att_slots[0] -= 1
                if hh % 2 == 1:
                    w_issue()

            def scores(h):
                hh = h % 4
                KHb, QNb, QRb = KH[h % 2], QN[h % 2], QR[h % 2]
                for (c0, n) in slabs:
                    q0 = t0 + c0
                    nk = (q0 + n - 1) // 128 + 1
                    bo, bd = ps_od()
                    blocks = []
                    for kt in range(nk):
                        qs = max(q0, 128 * kt)
                        blocks.append((kt, qs - q0, n - (qs - q0), 128 * kt >= q0))

                    def qk(i):
                        kt, cs_, N, diag = blocks[i]
                        b = ps_lo()
                        mm(psb(b, 0, N), KHb.v(0, kt * 128, kt * 128 + 128), QNb.v(0, c0 + cs_, c0 + cs_ + N), True, False)
                        mm(psb(b, 0, N), KR.v(0, kt * 128, kt * 128 + 128), QRb.v(0, c0 + cs_, c0 + cs_ + N), False, not diag)
                        if diag:
                            w_ = min(128, N)
                            mm(psb(b, 0, w_), identb, MSK.v(0, 0, w_), False, True)
                        return b
                    sb = [qk(0)]
                    if len(blocks) > 1:
                        sb.append(qk(1))
                    for i, (kt, cs_, N, diag) in enumerate(blocks):
                        if i + 2 < len(blocks):
                            sb.append(qk(i + 2))
                        bs = sb[i]
                        pt = PT[i % 3].v(0, 0, N)
                        act(pt, psb(bs, 0, N), AF.Exp, scale=SCALE)
                        mm(psb(bo, cs_, cs_ + N), VG.v(kt, hh * 128, hh * 128 + 128), pt, i == 0, i == len(blocks) - 1, signal=True)
                        mm(psb(bd, cs_, cs_ + N), ones, pt, i == 0, i == len(blocks) - 1, signal=True)
                    rd = TMPS[rd_i[0] % 4].v(0, 0, n)
                    rd_i[0] += 1
                    act(rd, psb(bd, 0, n), AF.Ln)
                    act(B4.v(h, c0, c0 + n), psb(bo, 0, n), AF.Copy)
                    act(rd, rd, AF.Exp, scale=-1.0)
                    tt(B4.v(h, c0, c0 + n), B4.v(h, c0, c0 + n), rd, ALU.mult)
                    drain(-(-len(bgq) // max(1, att_slots[0])))
                    att_slots[0] -= 1

            prepare(0)
            for h in range(NH):
                pre = h + 1 < NH and (h + 1) % 4 != 0
                if pre:
                    prepare(h + 1)
                scores(h)
                if h + 1 < NH and not pre:
                    prepare(h + 1)

            S.stage = "FE"
            def mix_stage(wname, gname, bname, src_fn, first):
                for j in range(16):
                    wa = w_next((wname, j))
                    ba = ps_next(NS)
                    gemm16(wa, src_fn, slabs, ba)
                    w_issue()
                    wg = w_next(("in", gname, j))
                    bg = ps_next(NS)
                    gemm16(wg, u_rhs, slabs, bg)
                    w_issue()
                    for si, (c0, n) in enumerate(slabs):
                        sg = TMPS[si % 2].v(0, 0, n)
                        act(sg, psb(bg[si], 0, n), AF.Sigmoid, bias=vcol(bname, j))
                        if first:
                            tt(B3.v(j, c0, c0 + n), psb(ba[si], 0, n), sg, ALU.mult)
                        else:
                            t1 = TMPS[2 + si % 2].v(0, 0, n)
                            stt(t1, psb(ba[si], 0, n), vcol("b_co", j), sg, ALU.add, ALU.mult)
                            tt(B3.v(j, c0, c0 + n), B3.v(j, c0, c0 + n), t1, ALU.add)
                    if first:
                        drain(-(-len(bgq) // max(1, 16 - j)))
            mix_stage("ao", "ga", "b_ga", lambda kc, c0, n: B4.v(kc, c0, c0 + n), True)
            drain()

            S.stage = "C2"
            for (c0, n) in slabs:
                (bm,) = ps_next(1)
                (bq,) = ps_next(1)
                for ch in range(16):
                    sq = TMPS[ch % 2]
                    sqv = R(sq.ap[:, 0, 0:n].bitcast(BF16)[:, 0:n], sq._keys(0, 1, 0, n))
                    cv = CB.v(ch, 32 + c0, 32 + c0 + n)
                    act(sqv, cv, AF.Square)
                    mm(psb(bm, 0, n), ones, cv, ch == 0, ch == 15, signal=True)
                    mm(psb(bq, 0, n), ones, sqv, ch == 0, ch == 15, signal=True)
                mean = MEAN.v(0, c0, c0 + n)
                var = VAR.v(0, c0, c0 + n)
                rstd = RSTD.v(0, c0, c0 + n)
                tsc(mean, psb(bm, 0, n), 1.0 / D, None, ALU.mult)
                tt(var, mean, mean, ALU.mult)
                stt(var, psb(bq, 0, n), 1.0 / D, var, ALU.mult, ALU.subtract)
                act(rstd, var, AF.Sqrt, bias=eps_c, scale=1.0)
                recip(rstd, rstd)
                for ch in range(16):
                    t1 = TMPS[2 + ch % 2].v(0, 0, n)
                    cv = CB.v(ch, 32 + c0, 32 + c0 + n)
                    tt(t1, cv, mean, ALU.subtract)
                    stt(t1, t1, vcol("g_ln", ch), rstd, ALU.mult, ALU.mult)
                    act(cv, t1, AF.Silu, bias=vcol("b_ln", ch))

            S.stage = "FE"
            mix_stage("co", "gc", "b_gc", lambda kc, c0, n: CB.v(kc, 32 + c0, 32 + c0 + n), False)

            S.stage = "G"
            for j in range(nblk):
                rows = min(128, T - j * 128)
                xs = XS[xs_use[0] % 2]
                xsem = "x%d" % (xs_use[0] % 2)
                xs_use[0] += 1
                if rows < 128:
                    vmemset(xs.v(0), 0.0)
                dma("sp", xs.v(0, 0, D, 0, rows), R(h0[t0 + j * 128: t0 + j * 128 + rows, :]), xsem)
                for q4 in range(4):
                    (b,) = ps_next(1)
                    for c in range(4):
                        ch = q4 * 4 + c
                        tr(psb(b, c * 128, (c + 1) * 128), xs.v(0, ch * 128, (ch + 1) * 128), ident_f)
                    src = R(ps_t[:, b, :].rearrange("p (c w) -> p c w", c=4), [("ps", b)])
                    copy_any(H.v3(q4 * 4, q4 * 4 + 4, j * 128, (j + 1) * 128), src)
            for j in range(16):
                ws = w_next(("wo", j))
                banks = ps_next(NS)
                gemm16(ws, lambda kc, c0, n: B3.v(kc, c0, c0 + n), slabs, banks)
                w_issue()
                for si, (c0, n) in enumerate(slabs):
                    tt(H.v(j, c0, c0 + n), psb(banks[si], 0, n), H.v(j, c0, c0 + n), ALU.add)

            S.stage = "FFN"
            sumsq_rstd(lambda ch, c0, n: H.v(ch, c0, c0 + n), 16, slabs, float(D))
            for ch in range(16):
                for (c0, n) in slabs:
                    stt(B3.v(ch, c0, c0 + n), H.v(ch, c0, c0 + n), vcol("g_ffn", ch), RSTD.v(0, c0, c0 + n), ALU.mult, ALU.mult)
            next_A = make_A(ti + 1) if ti + 1 < len(tiles) else []

            def hn_rhs(kc, c0, n):
                return B3.v(kc, c0, c0 + n)
            for g in range(NFG):
                ag = ACTG[g % 2]
                for f in range(4):
                    wg = w_next(("fg", g * 4 + f))
                    bg = ps_next(NS)
                    gemm16(wg, hn_rhs, slabs, bg)
                    w_issue()
                    wu = w_next(("fu", g * 4 + f))
                    bu = ps_next(NS)
                    gemm16(wu, hn_rhs, slabs, bu)
                    w_issue()
                    for si, (c0, n) in enumerate(slabs):
                        sg = TMPS[si % 2].v(0, 0, n)
                        act(sg, psb(bg[si], 0, n), AF.Silu)
                        tt(ag.v(f, c0, c0 + n), psb(bu[si], 0, n), sg, ALU.mult)
                if g >= 2 and next_A and not NO_A_OVERLAP:
                    next_A.pop(0)()
                    S.stage = "FFN"
                for cb in range(4):
                    wd = w_next(("fd", g, cb))
                    for o4 in range(4):
                        oc = cb * 4 + o4
                        banks = ps_next(NS)
                        for kc in range(4):
                            for si, (c0, n) in enumerate(slabs):
                                mm(psb(banks[si], 0, n), wv4(wd, kc, o4 * 128, o4 * 128 + 128), ag.v(kc, c0, c0 + n), kc == 0, kc == 3)
                        for si, (c0, n) in enumerate(slabs):
                            tt(H.v(oc, c0, c0 + n), psb(banks[si], 0, n), H.v(oc, c0, c0 + n), ALU.add)
                    w_issue()
            while next_A:
                next_A.pop(0)()

            pend_out = make_OUT(ti)
            if ti + 1 == len(tiles) or NO_OUT_OVERLAP:
                while pend_out:
                    side()

        assert wk["k"] == total_blocks and wk["issued"] == total_blocks, (wk, total_blocks)

        dma_sems = sorted(S.dma_count.keys())
        sem_names = list(S.ops.keys()) + dma_sems
        semh = {n: es.enter_context(nc.semaphore("s_" + n)) for n in sem_names}
        for k, v in S.maxwait.items():
            lim = S.count[k] if k in S.count else S.dma_count[k]
            assert v <= lim, ("wait beyond final count", k, v, lim)
        block = es.enter_context(nc.Block())

        def run(e, name):
            for fn, waits, signal, dsem in S.ops[name]:
                for (k, v) in waits:
                    e.wait_ge(semh[k], v)
                ins = fn(e)
                if signal:
                    ins.then_inc(semh[name], 1)
                if dsem is not None:
                    ins.then_inc(semh[dsem], 16)
            if name == "sp":
                for k in dma_sems:
                    if k.startswith("x") or k.startswith("c") or k.startswith("d"):
                        e.wait_ge(semh[k], S.dma_count[k])

        @block.tensor
        def _(e):
            run(e, "pe")

        @block.scalar
        def _(e):
            run(e, "act")

        @block.vector
        def _(e):
            run(e, "dve")

        @block.gpsimd
        def _(e):
            run(e, "pool")

        @block.sync
        def _(e):
            run(e, "sp")
    stats = {n: len(v) for n, v in S.ops.items()}
    stats['tags'] = S.tags
    return nc, stats


_CACHE = {}


def kernel(x, meta_tokens, g_mix, w_in, b_glu, b_gate, w_dw, b_dw, g_conv_ln, b_conv_ln, w_conv_out, b_conv_out,
           g_q_lora, w_uq, g_kv_lora, w_uk, w_uv, w_attn_out, w_out, g_ffn, w_ffn_gate, w_ffn_up, w_ffn_down, g_final):
    f = lambda a: np.asarray(a, np.float32)
    x = f(x)
    meta = f(meta_tokens)
    wst = pack_weights(f(w_in)[0], f(w_conv_out)[0], f(w_uq)[0], f(w_uk)[0], f(w_uv)[0], f(w_attn_out)[0], f(w_out)[0],
                       f(w_ffn_gate)[0], f(w_ffn_up)[0], f(w_ffn_down)[0])
    vecs = pack_vecs(f(b_glu)[0], f(b_gate)[0], f(b_dw)[0], f(g_conv_ln)[0], f(b_conv_ln)[0], f(b_conv_out)[0],
                     f(g_q_lora)[0], f(g_kv_lora)[0], f(g_ffn)[0], f(g_final), f(w_dw)[0])
    consts, cs = const_tables()
    gmix = np.ascontiguousarray(f(g_mix)[0])
    if "nc" not in _CACHE:
        _CACHE["nc"] = build_program()[0]
    nc = _CACHE["nc"]
    nb = x.shape[0]
    in_maps = []
    for b in range(nb):
        h0 = np.ascontiguousarray(np.concatenate([meta, x[b]], axis=0))
        in_maps.append({"h0": h0, "wst": wst, "vecs": vecs, "consts": consts, "cs": cs, "gmix": gmix})
    res = run_bass_kernel_spmd(nc, in_maps, core_ids=list(range(nb)))
    return np.stack([np.asarray(r["out"], np.float32) for r in res.results], axis=0)
```
